# Optimizing a Trainium2 kernel written in Bass

```python
import jax, jax.numpy as jnp
from jax import lax
import numpy as np

D_MODEL = 1024
BATCH = 32
SEQ = 256
DEPTH = 2
DEC_BATCH = 2
DEC_SEQ = 2048
PAST_LEN = 512

GRID_W = 64
EPS = 1e-6
N_BRANCH = 3
D_RNN = 1024
LRU_HEADS = 8
LRU_BLOCK = D_RNN // LRU_HEADS
LRU_CONV_W = 4
LRU_C = 8.0
N_HEADS = 8
N_KV_HEADS = 2
KV_GROUPS = N_HEADS // N_KV_HEADS
HEAD_DIM = 128
D_ATTN = N_HEADS * HEAD_DIM
D_KV = N_KV_HEADS * HEAD_DIM
WINDOW = 128
BLOCK_Q = 128
ROPE_BASE = 10000.0
NEG_INF = -1e30
D_POOL = 1024
POOL_WINDOWS = (2, 4, 8, 16)
POOL_GROUP = D_POOL // len(POOL_WINDOWS)
D_IN = D_RNN + D_ATTN + 2 * D_KV + D_POOL + N_BRANCH * D_MODEL
D_FF = 2816
FFN_CONV_W = 3

kernel_name = 'hybrid_flow_prefix_trunk_step'


def rmsnorm(x, g):
    xf = x.astype(jnp.float32)
    xf = xf * lax.rsqrt(jnp.mean(xf * xf, axis=-1, keepdims=True) + EPS)
    return xf.astype(x.dtype) * g


def dwconv(x, w, b, pad_left):
    T = x.shape[1]
    width = w.shape[0]
    xp = jnp.pad(x, ((0, 0), (pad_left, width - 1 - pad_left), (0, 0)))
    y = xp[:, 0:T] * w[0]
    for k in range(1, width):
        y = y + xp[:, k:k + T] * w[k]
    return y + b


def linear_scan(a, b, h0):
    b = b.at[:, 0].add(a[:, 0] * h0)

    def combine(left, right):
        a_l, b_l = left
        a_r, b_r = right
        return a_l * a_r, a_r * b_l + b_r

    _, h = lax.associative_scan(combine, (a, b), axis=1)
    return h


def rglru_direction(x, wa, ba, wx, bx, lam, h0):
    B, T, _ = x.shape
    xb = x.reshape(B, T, LRU_HEADS, LRU_BLOCK)
    r = jax.nn.sigmoid(jnp.einsum('bthi,hij->bthj', xb, wa).reshape(B, T, D_RNN) + ba)
    i = jax.nn.sigmoid(jnp.einsum('bthi,hij->bthj', xb, wx).reshape(B, T, D_RNN) + bx)
    log_a = -LRU_C * r * jax.nn.softplus(-lam)
    a = jnp.exp(log_a)
    b = jnp.sqrt(-jnp.expm1(2.0 * log_a)) * (i * x)
    return linear_scan(a, b, h0)


def rglru_bidir(x, p, h0):
    xf = x.astype(jnp.float32)
    h0 = h0.astype(jnp.float32)
    hf = rglru_direction(xf, p['lru_wa'][0], p['lru_ba'][0], p['lru_wx'][0], p['lru_bx'][0],
                         p['lru_lambda'][0], h0[:, 0])
    hb = jnp.flip(rglru_direction(jnp.flip(xf, 1), p['lru_wa'][1], p['lru_ba'][1], p['lru_wx'][1],
                                  p['lru_bx'][1], p['lru_lambda'][1], h0[:, 1]), 1)
    final = jnp.stack([hf[:, -1], hb[:, 0]], axis=1)
    return (hf + hb).astype(x.dtype), final.astype(x.dtype)


def rope_half(x, pos):
    nf = x.shape[-1] // 2
    freqs = ROPE_BASE ** (-jnp.arange(nf, dtype=jnp.float32) / nf)
    ang = pos.astype(jnp.float32)[:, None] * freqs[None, :]
    cos = jnp.cos(ang)[None, :, None, :]
    sin = jnp.sin(ang)[None, :, None, :]
    xf = x.astype(jnp.float32)
    x1, x2 = xf[..., :nf], xf[..., nf:]
    return jnp.concatenate([x1 * cos - x2 * sin, x1 * sin + x2 * cos], axis=-1)


def axial_rope(x):
    T = x.shape[1]
    rows = T // GRID_W
    row, col = jnp.meshgrid(jnp.arange(rows), jnp.arange(GRID_W), indexing='ij')
    half = x.shape[-1] // 2
    out = jnp.concatenate([rope_half(x[..., :half], row.reshape(-1)),
                           rope_half(x[..., half:], col.reshape(-1))], axis=-1)
    return out.astype(x.dtype)


def sink_logits(sink, B):
    return jnp.broadcast_to(sink.astype(jnp.float32).reshape(1, N_KV_HEADS, KV_GROUPS, 1, 1),
                            (B, N_KV_HEADS, KV_GROUPS, BLOCK_Q, 1))


def attn_context(q, k, v, sink):
    B, S = q.shape[:2]
    scale = HEAD_DIM ** -0.5
    sink_b = sink_logits(sink, B)

    def block(j):
        qj = lax.dynamic_slice_in_dim(q, j * BLOCK_Q, BLOCK_Q, axis=1)
        qj = qj.reshape(B, BLOCK_Q, N_KV_HEADS, KV_GROUPS, HEAD_DIM)
        s = jnp.einsum('bqkgd,bckd->bkgqc', qj, k).astype(jnp.float32) * scale
        pr = jax.nn.softmax(jnp.concatenate([s, sink_b], axis=-1), axis=-1)[..., :S]
        o = jnp.einsum('bkgqc,bckd->bqkgd', pr.astype(v.dtype), v)
        return o.reshape(B, BLOCK_Q, D_ATTN)

    out = lax.map(block, jnp.arange(S // BLOCK_Q))
    return jnp.transpose(out, (1, 0, 2, 3)).reshape(B, S, D_ATTN)


def attn_latent(q, k, v, kc, vc, sink):
    B, T = q.shape[:2]
    Lc = kc.shape[1]
    span = BLOCK_Q + 2 * WINDOW
    scale = HEAD_DIM ** -0.5
    kp = jnp.pad(k, ((0, 0), (WINDOW, WINDOW), (0, 0), (0, 0)))
    vp = jnp.pad(v, ((0, 0), (WINDOW, WINDOW), (0, 0), (0, 0)))
    sink_b = sink_logits(sink, B)

    def block(j):
        start = j * BLOCK_Q
        qj = lax.dynamic_slice_in_dim(q, start, BLOCK_Q, axis=1)
        qj = qj.reshape(B, BLOCK_Q, N_KV_HEADS, KV_GROUPS, HEAD_DIM)
        kj = lax.dynamic_slice_in_dim(kp, start, span, axis=1)
        vj = lax.dynamic_slice_in_dim(vp, start, span, axis=1)
        qpos = start + jnp.arange(BLOCK_Q)
        kpos = start - WINDOW + jnp.arange(span)
        valid = ((jnp.abs(qpos[:, None] - kpos[None, :]) <= WINDOW)
                 & (kpos >= 0)[None, :] & (kpos < T)[None, :])
        s_w = jnp.einsum('bqkgd,bwkd->bkgqw', qj, kj).astype(jnp.float32) * scale
        s_w = jnp.where(valid, s_w, NEG_INF)
        s_c = jnp.einsum('bqkgd,bckd->bkgqc', qj, kc).astype(jnp.float32) * scale
        pr = jax.nn.softmax(jnp.concatenate([s_w, s_c, sink_b], axis=-1), axis=-1)
        o = (jnp.einsum('bkgqw,bwkd->bqkgd', pr[..., :span].astype(vj.dtype), vj)
             + jnp.einsum('bkgqc,bckd->bqkgd', pr[..., span:span + Lc].astype(vc.dtype), vc))
        return o.reshape(B, BLOCK_Q, D_ATTN)

    out = lax.map(block, jnp.arange(T // BLOCK_Q))
    return jnp.transpose(out, (1, 0, 2, 3)).reshape(B, T, D_ATTN)


def multiscale_pool(x, w, scale):
    B, T, _ = x.shape
    xf = x.astype(jnp.float32)
    csum = jnp.concatenate([jnp.zeros((B, 1, D_POOL), jnp.float32), jnp.cumsum(xf, axis=1)], axis=1)
    t = jnp.arange(T)
    parts = []
    for gi, win in enumerate(POOL_WINDOWS):
        c0, c1 = gi * POOL_GROUP, (gi + 1) * POOL_GROUP
        lo = jnp.clip(t - win // 2, 0, T)
        hi = jnp.clip(t + win // 2, 0, T)
        cnt = (hi - lo).astype(jnp.float32)[None, :, None]
        mean = (csum[:, hi, c0:c1] - csum[:, lo, c0:c1]) / cnt
        parts.append(mean - xf[..., c0:c1])
    pooled = jnp.stack(parts, axis=2).astype(x.dtype)
    y = jnp.einsum('btgi,gij->btgj', pooled, w).reshape(B, T, D_POOL)
    return y * scale


def trunk_layer(x, mod, p, ctx_k, ctx_v, ctx_h):
    latent = ctx_k is not None
    B, T, _ = x.shape
    shift1, scale1, gate1, shift2, scale2, gate2 = jnp.split(mod, 6, axis=-1)
    h = rmsnorm(x, p['norm1']) * (1.0 + scale1) + shift1
    proj = h @ p['w_in']
    s0 = D_RNN
    s1 = s0 + D_ATTN
    s2 = s1 + D_KV
    s3 = s2 + D_KV
    s4 = s3 + D_POOL
    xa, q, k, v, xc, g = (proj[..., :s0], proj[..., s0:s1], proj[..., s1:s2],
                          proj[..., s2:s3], proj[..., s3:s4], proj[..., s4:])
    xa = dwconv(xa, p['lru_conv'], p['lru_conv_b'], LRU_CONV_W // 2)
    h0 = ctx_h if latent else jnp.zeros((B, 2, D_RNN), x.dtype)
    ya, h_final = rglru_bidir(xa, p, h0)
    q = q.reshape(B, T, N_HEADS, HEAD_DIM)
    k = k.reshape(B, T, N_KV_HEADS, HEAD_DIM)
    v = v.reshape(B, T, N_KV_HEADS, HEAD_DIM)
    if latent:
        yb = attn_latent(axial_rope(q), axial_rope(k), v, ctx_k, ctx_v, p['attn_sink'])
    else:
        yb = attn_context(q, k, v, p['attn_sink'])
    yc = multiscale_pool(xc, p['pool_w'], p['pool_scale'])
    gates = jax.nn.sigmoid(g + p['b_gate']).reshape(B, T, N_BRANCH, D_MODEL)
    merged = (gates[:, :, 0] * (ya @ p['w_branch'][0])
              + gates[:, :, 1] * (yb @ p['w_branch'][1])
              + gates[:, :, 2] * (yc @ p['w_branch'][2]))
    x = x + gate1 * (merged @ p['w_out'])
    h2 = rmsnorm(x, p['norm2']) * (1.0 + scale2) + shift2
    up = h2 @ p['ffn_up']
    gff = dwconv(up[..., :D_FF], p['ffn_conv'], p['ffn_conv_b'], FFN_CONV_W // 2)
    x = x + gate2 * ((jax.nn.gelu(gff) * up[..., D_FF:]) @ p['ffn_down'])
    return x, k, v, h_final


def setup_inputs(seed: int = 0) -> dict:
    key = jax.random.key(seed)
    ks = iter(jax.random.split(key, 32))

    def nrm(shape, s):
        return jax.random.normal(next(ks), shape, jnp.float32) * s

    u = jax.random.uniform(next(ks), (DEPTH, 2, D_RNN), jnp.float32, minval=0.9, maxval=0.999)
    a0 = u ** (1.0 / LRU_C)
    lru_lambda = jnp.log(a0) - jnp.log1p(-a0)
    return {
        'x_prompt': nrm((BATCH, SEQ, D_MODEL), 1.0),
        'x_sample': nrm((DEC_BATCH, DEC_SEQ, D_MODEL), 1.0),
        'cache_k': nrm((DEC_BATCH, DEPTH, PAST_LEN, N_KV_HEADS, HEAD_DIM), 1.0),
        'cache_v': nrm((DEC_BATCH, DEPTH, PAST_LEN, N_KV_HEADS, HEAD_DIM), 1.0),
        'state_lru': nrm((DEC_BATCH, DEPTH, 2, D_RNN), 0.5),
        'c': nrm((DEC_BATCH, D_MODEL), 1.0),
        'c_ctx': nrm((D_MODEL,), 1.0),
        'w_ada': nrm((DEPTH, D_MODEL, 6 * D_MODEL), 0.5 * D_MODEL ** -0.5),
        'b_ada': nrm((DEPTH, 6 * D_MODEL), 0.02),
        'norm1': 1.0 + nrm((DEPTH, D_MODEL), 0.05),
        'norm2': 1.0 + nrm((DEPTH, D_MODEL), 0.05),
        'w_in': nrm((DEPTH, D_MODEL, D_IN), D_MODEL ** -0.5),
        'b_gate': nrm((DEPTH, N_BRANCH * D_MODEL), 0.1),
        'lru_conv': nrm((DEPTH, LRU_CONV_W, D_RNN), LRU_CONV_W ** -0.5),
        'lru_conv_b': nrm((DEPTH, D_RNN), 0.02),
        'lru_wa': nrm((DEPTH, 2, LRU_HEADS, LRU_BLOCK, LRU_BLOCK), LRU_BLOCK ** -0.5),
        'lru_ba': nrm((DEPTH, 2, D_RNN), 0.1),
        'lru_wx': nrm((DEPTH, 2, LRU_HEADS, LRU_BLOCK, LRU_BLOCK), LRU_BLOCK ** -0.5),
        'lru_bx': nrm((DEPTH, 2, D_RNN), 0.1),
        'lru_lambda': lru_lambda,
        'attn_sink': nrm((DEPTH, N_HEADS), 0.5),
        'pool_w': nrm((DEPTH, len(POOL_WINDOWS), POOL_GROUP, POOL_GROUP), POOL_GROUP ** -0.5),
        'pool_scale': 1.0 + nrm((DEPTH, D_POOL), 0.05),
        'w_branch': nrm((DEPTH, N_BRANCH, D_MODEL, D_MODEL), D_MODEL ** -0.5),
        'w_out': nrm((DEPTH, D_MODEL, D_MODEL), D_MODEL ** -0.5),
        'ffn_up': nrm((DEPTH, D_MODEL, 2 * D_FF), D_MODEL ** -0.5),
        'ffn_conv': nrm((DEPTH, FFN_CONV_W, D_FF), FFN_CONV_W ** -0.5),
        'ffn_conv_b': nrm((DEPTH, D_FF), 0.02),
        'ffn_down': nrm((DEPTH, D_FF, D_MODEL), D_FF ** -0.5),
        'final_norm': 1.0 + nrm((D_MODEL,), 0.05),
    }


def reference(x_prompt, x_sample, cache_k, cache_v, state_lru, c, c_ctx, w_ada, b_ada, norm1, norm2,
              w_in, b_gate, lru_conv, lru_conv_b, lru_wa, lru_ba, lru_wx, lru_bx, lru_lambda, attn_sink,
              pool_w, pool_scale, w_branch, w_out, ffn_up, ffn_conv, ffn_conv_b, ffn_down, final_norm):
    xp = x_prompt
    xs = x_sample
    ks, vs, hs = [], [], []
    for l in range(DEPTH):
        p = {
            'norm1': norm1[l], 'norm2': norm2[l], 'w_in': w_in[l], 'b_gate': b_gate[l],
            'lru_conv': lru_conv[l], 'lru_conv_b': lru_conv_b[l], 'lru_wa': lru_wa[l], 'lru_ba': lru_ba[l],
            'lru_wx': lru_wx[l], 'lru_bx': lru_bx[l], 'lru_lambda': lru_lambda[l], 'attn_sink': attn_sink[l],
            'pool_w': pool_w[l], 'pool_scale': pool_scale[l], 'w_branch': w_branch[l], 'w_out': w_out[l],
            'ffn_up': ffn_up[l], 'ffn_conv': ffn_conv[l], 'ffn_conv_b': ffn_conv_b[l], 'ffn_down': ffn_down[l],
        }
        mod_ctx = jax.nn.silu(c_ctx) @ w_ada[l] + b_ada[l]
        mod_lat = (jax.nn.silu(c) @ w_ada[l] + b_ada[l])[:, None, :]
        xp, k_l, v_l, h_l = trunk_layer(xp, mod_ctx, p, None, None, None)
        xs, _, _, _ = trunk_layer(xs, mod_lat, p, cache_k[:, l], cache_v[:, l], state_lru[:, l])
        ks.append(k_l)
        vs.append(v_l)
        hs.append(h_l)
    y_prompt = rmsnorm(xp, final_norm)
    y_sample = rmsnorm(xs, final_norm)
    new_cache_k = jnp.stack(ks, axis=1)
    new_cache_v = jnp.stack(vs, axis=1)
    new_state_lru = jnp.stack(hs, axis=1)
    return (y_prompt, y_sample, new_cache_k, new_cache_v, new_state_lru)
```

```python
import contextlib
import numpy as np
import concourse.bass as bass
import concourse.mybir as mybir
from concourse.bass_utils import run_bass_kernel_spmd

F32 = mybir.dt.float32
BF16 = mybir.dt.bfloat16
AF = mybir.ActivationFunctionType
ALU = mybir.AluOpType

D = 1024
NCH = 8
DEPTH = 2
D_IN = 6656
D_FF = 2816
NFF = 22
EPS = 1e-6
TB = 512
NSEM_DMA = 12
NWBUF = 4
WBUF_EL = 4096


class CFG:
    def __init__(self, np_seq=4, lp=256, ts=2048, past=512, grid_w=64, depth=DEPTH):
        self.np_seq = np_seq
        self.lp = lp
        self.tp = np_seq * lp
        self.ts = ts
        self.past = past
        self.grid_w = grid_w
        self.depth = depth


VEC_SPEC = [("norm1", 8), ("norm2", 8), ("b_ada", 48), ("conv0", 8), ("conv1", 8), ("conv2", 8), ("conv3", 8),
            ("conv_b", 8), ("ba0", 8), ("ba1", 8), ("bx0", 8), ("bx1", 8), ("lam0", 8), ("lam1", 8),
            ("b_gate", 24), ("pool_scale", 8), ("fconv0", 22), ("fconv1", 22), ("fconv2", 22), ("fconv_b", 22),
            ("final_norm", 8)]
VEC_OFF = {}
_o = 0
for _n, _w in VEC_SPEC:
    VEC_OFF[_n] = (_o, _w)
    _o += _w
NV = _o


def _fm(v):
    v = np.asarray(v, np.float32)
    return np.ascontiguousarray(v.reshape(-1, 128).T)


_RETIRED = {}


class Res:
    __slots__ = ("w", "r", "name")

    def __init__(self, name=""):
        self.w = None
        self.r = dict(_RETIRED)
        self.name = name


class Eng:
    def __init__(self, name, h, sem):
        self.name = name
        self.h = h
        self.sem = sem
        self.cnt = 0
        self.waited = {}
        self.pr = []
        self.pw = []


class KB:
    def __init__(self, nc, es):
        self.nc = nc
        self.es = es
        self.engs = {}
        for n in ["tensor", "vector", "scalar", "gpsimd", "sync"]:
            self.engs[n] = Eng(n, getattr(nc, n), es.enter_context(nc.semaphore("sem_" + n)))
        self.dsem = {}
        self.dcnt = {}
        for q in ["gpsimd", "sync"]:
            self.dsem[q] = [es.enter_context(nc.semaphore("dsem_%s_%d" % (q, i))) for i in range(NSEM_DMA)]
            self.dcnt[q] = 0
        self.out_toks = []
        self.scoped_dma = {}
        _RETIRED.clear()

    def retire(self):
        e = self.engs["tensor"]
        assert not e.pr and not e.pw
        for en in self.engs.values():
            if en.cnt > 0:
                _RETIRED[id(en.sem)] = (en.sem, en.cnt, None)
        for k, t in self.scoped_dma.items():
            _RETIRED[k] = t

    def _need(self, eng, tok):
        if tok is None:
            return
        sem, val, src = tok
        if src is eng:
            if eng.name == "tensor":
                return
            if eng.cnt - val >= 2:
                return
        k = id(sem)
        if eng.waited.get(k, 0) >= val:
            return
        eng.h.wait_ge(sem, val)
        eng.waited[k] = val

    def _deps(self, eng, reads, writes):
        for r in reads:
            self._need(eng, r.w)
        for w in writes:
            self._need(eng, w.w)
            for t in w.r.values():
                self._need(eng, t)

    def _commit(self, tok, reads, writes):
        k = id(tok[0])
        for r in reads:
            r.r[k] = tok
        for w in writes:
            w.w = tok
            w.r = {}

    def op(self, engname, fn, reads=(), writes=()):
        eng = self.engs[engname]
        self._deps(eng, reads, writes)
        inst = fn(eng.h)
        eng.cnt += 1
        inst.then_inc(eng.sem, 1)
        tok = (eng.sem, eng.cnt, eng)
        self._commit(tok, reads, writes)
        return tok

    def V(self, fn, reads=(), writes=()):
        return self.op("vector", fn, reads, writes)

    def A(self, fn, reads=(), writes=()):
        return self.op("scalar", fn, reads, writes)

    def mm(self, out, lhsT, rhs, start, stop, reads=(), writes=(), inc=None):
        if inc is None:
            inc = stop
        eng = self.engs["tensor"]
        self._deps(eng, reads, writes)
        inst = eng.h.matmul(out, lhsT=lhsT, rhs=rhs, start=start, stop=stop)
        eng.pr.extend(reads)
        eng.pw.extend(writes)
        if inc:
            eng.cnt += 1
            inst.then_inc(eng.sem, 1)
            tok = (eng.sem, eng.cnt, eng)
            self._commit(tok, eng.pr, eng.pw)
            eng.pr = []
            eng.pw = []

    def dma(self, q, out, in_, reads=(), writes=(), is_out=False, scoped=True, **kw):
        eng = self.engs[q]
        self._deps(eng, reads, writes)
        j = self.dcnt[q]
        self.dcnt[q] += 1
        sem = self.dsem[q][j % NSEM_DMA]
        val = 16 * (j // NSEM_DMA + 1)
        if j >= NSEM_DMA:
            self._need(eng, (sem, val - 16, None))
        eng.h.dma_start(out=out, in_=in_, **kw).then_inc(sem, 16)
        tok = (sem, val, None)
        self._commit(tok, reads, writes)
        if scoped:
            self.scoped_dma[id(sem)] = tok
        if is_out:
            self.out_toks.append(tok)
        return tok

    def all_tokens(self):
        toks = []
        for e in self.engs.values():
            if e.cnt > 0:
                toks.append((e.sem, e.cnt, e))
        for q in self.dsem:
            n = self.dcnt[q]
            for i in range(min(n, NSEM_DMA)):
                last_j = i + ((n - 1 - i) // NSEM_DMA) * NSEM_DMA
                toks.append((self.dsem[q][i], 16 * (last_j // NSEM_DMA + 1), None))
        return toks

    def barrier(self):
        assert not self.engs["tensor"].pr and not self.engs["tensor"].pw
        toks = self.all_tokens()
        for e in self.engs.values():
            for t in toks:
                if t[2] is e:
                    if e.name != "tensor" and e.cnt - t[1] < 2:
                        e.h.wait_ge(t[0], t[1])
                        e.waited[id(t[0])] = t[1]
                    continue
                self._need(e, t)

    def finish(self):
        e = self.engs["sync"]
        for t in self.all_tokens():
            if t[2] is e:
                continue
            self._need(e, t)


_UNIQ = [0]


def _uniq(name):
    _UNIQ[0] += 1
    return "t%d_%s" % (_UNIQ[0], name)


def interleave(gens):
    res = [None] * len(gens)
    alive = list(range(len(gens)))
    while alive:
        for i in list(alive):
            try:
                next(gens[i])
            except StopIteration as e:
                res[i] = e.value
                alive.remove(i)
    return res


class Pool:
    def __init__(self, kb, scope, name, shape, dt, n, psum=False):
        self.tiles = []
        for i in range(n):
            if psum:
                t = scope.enter_context(kb.nc.psum_tensor(_uniq("%s%d" % (name, i)), shape, dt))
            else:
                t = scope.enter_context(kb.nc.sbuf_tensor(_uniq("%s%d" % (name, i)), shape, dt))
            self.tiles.append((t, Res("%s%d" % (name, i))))
        self.i = 0

    def next(self):
        t = self.tiles[self.i % len(self.tiles)]
        self.i += 1
        return t


class Builder:
    def __init__(self, cfg, dbg=None):
        self.cfg = cfg
        self.nc = bass.Bass("TRN2", target_bir_lowering=False)
        self.dbg = dbg
        self.dbg_out = {}

    def dump(self, name, ap, shape, res):
        o = self.nc.dram_tensor("dbg_" + name, list(shape), F32, kind="ExternalOutput").ap()
        self.dbg_out[name] = o
        self.kb.dma("gpsimd", o, ap, reads=res, is_out=True)

    def dram_in(self, name, shape):
        return self.nc.dram_tensor(name, list(shape), F32, kind="ExternalInput").ap()

    def dram_out(self, name, shape):
        return self.nc.dram_tensor(name, list(shape), F32, kind="ExternalOutput").ap()

    def sb(self, scope, name, shape, dt=F32):
        return scope.enter_context(self.nc.sbuf_tensor(_uniq(name), list(shape), dt))

    def wload(self, src2d, nel):
        t, r = self.wpool.next()
        dst = t[:, 0:nel]
        if nel > 1024 and nel % 1024 == 0:
            self.kb.dma("gpsimd", dst.rearrange("p (a b) -> p a b", b=1024),
                        src2d.rearrange("p (a b) -> p a b", b=1024), writes=[r], scoped=False)
        else:
            self.kb.dma("gpsimd", dst, src2d, writes=[r], scoped=False)
        return t, r

    def build(self):
        cfg = self.cfg
        nc = self.nc
        L = cfg.depth
        TP, TS, PAST = cfg.tp, cfg.ts, cfg.past
        d = {}
        d["xp"] = self.dram_in("xp", [D, TP])
        d["xs"] = self.dram_in("xs", [D, TS])
        d["cvec"] = self.dram_in("cvec", [128, NCH, 2])
        d["vecs"] = self.dram_in("vecs", [L, 128, NV])
        d["h0s"] = self.dram_in("h0s", [L, 128, 2, NCH])
        d["sink"] = self.dram_in("sink", [L, 128, 8])
        d["w_ada"] = self.dram_in("w_ada", [L, 12, 128, 4096])
        d["w_in"] = self.dram_in("w_in", [L, 7, 128, 4096])
        d["w_g"] = self.dram_in("w_g", [L, 8, 128, 3072])
        d["w_br"] = self.dram_in("w_br", [L, 8, 128, 3072])
        d["w_out"] = self.dram_in("w_out", [L, 2, 128, 4096])
        d["w_up"] = self.dram_in("w_up", [L, 11, 128, 4096])
        d["w_dn"] = self.dram_in("w_dn", [L, 8, 128, 2816])
        d["w_lru"] = self.dram_in("w_lru", [L, 8, 128, 512])
        d["w_pool"] = self.dram_in("w_pool", [L, 4, 128, 512])
        d["kcT"] = self.dram_in("kcT", [L, 128, 2 * PAST])
        d["vc"] = self.dram_in("vc", [L, 128, (PAST // 128) * 256])
        d["perm"] = self.dram_in("perm", [128, 128])
        d["masks"] = self.dram_in("masks", [128, 2 * 512])
        d["rope"] = self.dram_in("rope", [2, 128, TS])
        d["pedge"] = self.dram_in("pedge", [128, 4 * 16])
        d["yp"] = self.dram_out("yp", [D, TP])
        d["ys"] = self.dram_out("ys", [D, TS])
        d["ko"] = self.dram_out("ko", [L, TP, 256])
        d["vo"] = self.dram_out("vo", [L, TP, 256])
        d["ho"] = self.dram_out("ho", [128, L * 2 * NCH * cfg.np_seq])
        self.d = d

        with contextlib.ExitStack() as es:
            self.kb = kb = KB(nc, es)
            self.es = es
            self.psum = Pool(kb, es, "ps", [128, 512], F32, 4, psum=True)
            self.pacc = Pool(kb, es, "pa", [128, 512], F32, 4, psum=True)
            self.wpool = Pool(kb, es, "wb", [128, WBUF_EL], BF16, NWBUF)
            self.n_sq = Pool(kb, es, "nsq", [128, NCH, 512], BF16, 1)
            self.n_rs = Pool(kb, es, "nrs", [128, 512], F32, 2)
            self.n_tt = Pool(kb, es, "ntt", [128, 512], F32, 2)
            self.setup_consts(es)
            self.par = self.sb(es, "par", [128, L, 2, 6, NCH], F32)
            self.compute_mod(0)
            with contextlib.ExitStack() as gs:
                self.run_group(gs, "P")
                kb.retire()
            with contextlib.ExitStack() as gs:
                self.run_group(gs, "S")
                kb.retire()
            kb.finish()
        return nc

    def setup_consts(self, es):
        cfg, kb, nc, d = self.cfg, self.kb, self.nc, self.d
        L = cfg.depth
        self.R_c = Res("consts")
        Rc = self.R_c
        self.ones_m = self.sb(es, "ones_m", [128, 128], BF16)
        self.ones_1 = self.sb(es, "ones_1", [128, 128], BF16)
        kb.V(lambda e: e.memset(self.ones_m[:], 1.0 / D), writes=[Rc])
        kb.V(lambda e: e.memset(self.ones_1[:], 1.0), writes=[Rc])
        self.perm = self.sb(es, "perm", [128, 128], F32)
        kb.dma("sync", self.perm[:], d["perm"][:], writes=[Rc])
        self.masks = self.sb(es, "masks", [128, 2 * 512], BF16)
        kb.dma("gpsimd", self.masks[:], d["masks"][:], writes=[Rc])
        self.pedge = self.sb(es, "pedge", [128, 64], F32)
        kb.dma("sync", self.pedge[:], d["pedge"][:], writes=[Rc])
        self.vec = self.sb(es, "vec", [128, L, NV], F32)
        for l in range(L):
            kb.dma("sync", self.vec[:, l, :], d["vecs"][l], writes=[Rc])
        self.h0s = self.sb(es, "h0s", [128, L, 2, NCH], F32)
        for l in range(L):
            kb.dma("sync", self.h0s[:, l], d["h0s"][l], writes=[Rc])
        self.esink = self.sb(es, "esink", [128, L, 8], F32)
        for l in range(L):
            kb.dma("sync", self.esink[:, l, :], d["sink"][l], writes=[Rc])
        kb.A(lambda e: e.activation(out=self.esink[:], in_=self.esink[:], func=AF.Exp), reads=[Rc], writes=[Rc])
        self.der = self.sb(es, "der", [128, L, 8, NCH], F32)
        self.bgh = self.sb(es, "bgh", [128, L, 24], F32)
        tmp = self.sb(es, "lamtmp", [128, 2, NCH], F32)
        tmp2 = self.sb(es, "lamtmp2", [128, 2, NCH], F32)
        for l in range(L):
            for i, nm in enumerate(["ba0", "ba1", "bx0", "bx1"]):
                kb.V(lambda e, i=i, nm=nm: e.tensor_scalar(out=self.der[:, l, i, :], in0=self.vv(l, nm), scalar1=-1.0,
                                                           scalar2=None, op0=ALU.mult), reads=[Rc], writes=[Rc])
            kb.V(lambda e: e.tensor_scalar(out=self.bgh[:, l, :], in0=self.vv(l, "b_gate"), scalar1=0.5, scalar2=None,
                                           op0=ALU.mult), reads=[Rc], writes=[Rc])
            o0 = VEC_OFF["lam0"][0]
            lam = self.vec[:, l, o0:o0 + 16].rearrange("p (a b) -> p a b", a=2)
            kb.A(lambda e: e.activation(out=tmp[:], in_=lam, func=AF.Exp, scale=-1.0), reads=[Rc], writes=[Rc])
            kb.V(lambda e: e.tensor_scalar(out=tmp2[:], in0=tmp[:], scalar1=-0.25, scalar2=1.0 / 3.0, op0=ALU.mult,
                                           op1=ALU.add), reads=[Rc], writes=[Rc])
            kb.V(lambda e: e.tensor_tensor(out=tmp2[:], in0=tmp2[:], in1=tmp[:], op=ALU.mult), reads=[Rc], writes=[Rc])
            kb.V(lambda e: e.tensor_scalar(out=tmp2[:], in0=tmp2[:], scalar1=-1.0, scalar2=0.5, op0=ALU.mult,
                                           op1=ALU.add), reads=[Rc], writes=[Rc])
            kb.V(lambda e: e.tensor_tensor(out=tmp2[:], in0=tmp2[:], in1=tmp[:], op=ALU.mult), reads=[Rc], writes=[Rc])
            kb.V(lambda e: e.tensor_scalar(out=tmp2[:], in0=tmp2[:], scalar1=-1.0, scalar2=1.0, op0=ALU.mult,
                                           op1=ALU.add), reads=[Rc], writes=[Rc])
            kb.V(lambda e: e.tensor_tensor(out=tmp2[:], in0=tmp2[:], in1=tmp[:], op=ALU.mult), reads=[Rc], writes=[Rc])
            kb.V(lambda e: e.tensor_scalar(out=self.der[:, l, 4:6, :], in0=tmp2[:], scalar1=-8.0, scalar2=None,
                                           op0=ALU.mult), reads=[Rc], writes=[Rc])
            kb.V(lambda e: e.tensor_scalar(out=self.der[:, l, 6:8, :], in0=tmp2[:], scalar1=-16.0, scalar2=None,
                                           op0=ALU.mult), reads=[Rc], writes=[Rc])

    def vv(self, l, name):
        o, w = VEC_OFF[name]
        return self.vec[:, l, o:o + w]

    def vcol(self, l, name, c):
        o, w = VEC_OFF[name]
        return self.vec[:, l, o + c:o + c + 1]

    def compute_mod(self, l):
        cfg, kb, nc, d = self.cfg, self.kb, self.nc, self.d
        L = cfg.depth
        Rc = self.R_c
        with contextlib.ExitStack() as sc:
            cv = self.sb(sc, "cv", [128, NCH, 2], F32)
            cvb = self.sb(sc, "cvb", [128, NCH, 2], BF16)
            modT = self.sb(sc, "modT", [128, 48, 2], F32)
            Rm = Res("mod")
            kb.dma("sync", cv[:], d["cvec"][:], writes=[Rm])
            kb.A(lambda e: e.activation(out=cvb[:], in_=cv[:], func=AF.Silu), reads=[Rm], writes=[Rm])
            if True:
                pt, pr = self.psum.next()
                for st in range(12):
                    wt, wr = self.wload(d["w_ada"][l, st], 4096)
                    w3 = wt[:, 0:4096].rearrange("p (k n) -> p k n", k=NCH)
                    for cc in range(4):
                        j = st * 4 + cc
                        for kc in range(NCH):
                            kb.mm(pt[:, 2 * j:2 * j + 2], w3[:, kc, cc * 128:(cc + 1) * 128], cvb[:, kc, :],
                                  start=(kc == 0), stop=(kc == NCH - 1), reads=[wr, Rm], writes=[pr],
                                  inc=(kc == NCH - 1 and cc == 3))
                kb.A(lambda e: e.activation(out=modT[:].rearrange("p a b -> p (a b)"), in_=pt[:, 0:96], func=AF.Copy),
                     reads=[pr], writes=[Rm])
                for g in range(2):
                    kb.V(lambda e, g=g: e.tensor_tensor(out=modT[:, :, g], in0=modT[:, :, g], in1=self.vv(l, "b_ada"),
                                                        op=ALU.add), reads=[Rm, Rc], writes=[Rm])
                    P = self.par
                    kb.V(lambda e, g=g: e.scalar_tensor_tensor(out=P[:, l, g, 0, :], in0=modT[:, 8:16, g], scalar=1.0,
                                                               in1=self.vv(l, "norm1"), op0=ALU.add, op1=ALU.mult),
                         reads=[Rm, Rc], writes=[Rc])
                    kb.V(lambda e, g=g: e.tensor_copy(out=P[:, l, g, 1, :], in_=modT[:, 0:8, g]), reads=[Rm], writes=[Rc])
                    kb.V(lambda e, g=g: e.tensor_scalar(out=P[:, l, g, 2, :], in0=modT[:, 16:24, g], scalar1=0.5,
                                                        scalar2=None, op0=ALU.mult), reads=[Rm], writes=[Rc])
                    kb.V(lambda e, g=g: e.scalar_tensor_tensor(out=P[:, l, g, 3, :], in0=modT[:, 32:40, g], scalar=1.0,
                                                               in1=self.vv(l, "norm2"), op0=ALU.add, op1=ALU.mult),
                         reads=[Rm, Rc], writes=[Rc])
                    kb.V(lambda e, g=g: e.tensor_copy(out=P[:, l, g, 4, :], in_=modT[:, 24:32, g]), reads=[Rm], writes=[Rc])
                    kb.V(lambda e, g=g: e.tensor_copy(out=P[:, l, g, 5, :], in_=modT[:, 40:48, g]), reads=[Rm], writes=[Rc])
            kb.retire()

    def norm_cols(self, x, xR, lo, hi, A, B, out_fn, outR, pools, done_fn=None):
        self._norm_multi(x, [xR] if isinstance(xR, Res) else list(xR), lo, hi, A, B, out_fn, outR, pools, done_fn)

    def lru_piece(self, l, c, dr, lw, lwr, xcv, xcb, xR, lo, w, pools):
        kb = self.kb
        Rc = self.R_c
        pz, pzr = self.psum.next()
        kb.mm(pz[:, 0:w], lw[:, dr, 0, :], xcb[:, lo:lo + w], True, True, reads=[lwr, xR], writes=[pzr])
        pi, pir = self.psum.next()
        kb.mm(pi[:, 0:w], lw[:, dr, 1, :], xcb[:, lo:lo + w], True, True, reads=[lwr, xR], writes=[pir])
        yield
        er, err = pools["f512"].next()
        ei, eir = pools["f512"].next()
        dv = self.der
        kb.A(lambda e: e.activation(out=er[:, 0:w], in_=pz[:, 0:w], func=AF.Exp, scale=-1.0,
                                    bias=dv[:, l, 0 + dr, c:c + 1]), reads=[pzr, Rc], writes=[err])
        yield
        kb.A(lambda e: e.activation(out=ei[:, 0:w], in_=pi[:, 0:w], func=AF.Exp, scale=-1.0,
                                    bias=dv[:, l, 2 + dr, c:c + 1]), reads=[pir, Rc], writes=[eir])
        yield
        kb.A(lambda e: e.activation(out=er[:, 0:w], in_=er[:, 0:w], func=AF.Ln, bias=1.0),
             reads=[err, Rc], writes=[err])
        yield
        kb.A(lambda e: e.activation(out=er[:, 0:w], in_=er[:, 0:w], func=AF.Exp, scale=-1.0), reads=[err], writes=[err])
        yield
        kb.A(lambda e: e.activation(out=ei[:, 0:w], in_=ei[:, 0:w], func=AF.Ln, bias=1.0),
             reads=[eir, Rc], writes=[eir])
        yield
        at, ar = pools["f512"].next()
        st, sr = pools["f512"].next()
        kb.A(lambda e: e.activation(out=at[:, 0:w], in_=er[:, 0:w], func=AF.Exp, scale=dv[:, l, 4 + dr, c:c + 1]),
             reads=[err, Rc], writes=[ar])
        yield
        kb.V(lambda e: e.tensor_tensor(out=st[:, 0:w], in0=at[:, 0:w], in1=at[:, 0:w], op=ALU.mult), reads=[ar], writes=[sr])
        yield
        kb.A(lambda e: e.activation(out=st[:, 0:w], in_=st[:, 0:w], func=AF.Ln, scale=-1.0, bias=1.0),
             reads=[sr, Rc], writes=[sr])
        yield
        kb.V(lambda e: e.scalar_tensor_tensor(out=st[:, 0:w], in0=st[:, 0:w], scalar=0.5, in1=ei[:, 0:w],
                                              op0=ALU.mult, op1=ALU.subtract), reads=[sr, eir], writes=[sr])
        yield
        kb.A(lambda e: e.activation(out=st[:, 0:w], in_=st[:, 0:w], func=AF.Exp), reads=[sr], writes=[sr])
        yield
        kb.V(lambda e: e.tensor_tensor(out=ei[:, 0:w], in0=st[:, 0:w], in1=xcv[:, lo:lo + w], op=ALU.mult),
             reads=[sr, xR, eir], writes=[eir])
        yield
        return at, ar, ei, eir

    def conv_lru(self, l, c, xa_pad, xaR, nseg, Lb, xcv, xcb, xR):
        kb = self.kb
        Rc = self.R_c
        xo = xcv[:, 0:nseg * Lb].rearrange("p (s t) -> p s t", s=nseg)
        kb.V(lambda e: e.tensor_scalar(out=xo, in0=xa_pad[:, :, 0:Lb], scalar1=self.vcol(l, "conv0", c),
                                       scalar2=self.vcol(l, "conv_b", c), op0=ALU.mult, op1=ALU.add),
             reads=[xaR, Rc], writes=[xR])
        for k in range(1, 4):
            kb.V(lambda e, k=k: e.scalar_tensor_tensor(out=xo, in0=xa_pad[:, :, k:k + Lb],
                                                       scalar=self.vcol(l, "conv%d" % k, c), in1=xo,
                                                       op0=ALU.mult, op1=ALU.add), reads=[xaR, xR, Rc], writes=[xR])
        kb.V(lambda e: e.tensor_copy(out=xcb[:, 0:nseg * Lb], in_=xcv[:, 0:nseg * Lb]), reads=[xR], writes=[xR])

    def run_group(self, gs, G):
        cfg, kb, nc, d = self.cfg, self.kb, self.nc, self.d
        L = cfg.depth
        isS = (G == "S")
        T = cfg.ts if isS else cfg.tp
        g = 1 if isS else 0
        nblk = T // TB
        Rc = self.R_c
        if not hasattr(self, "eps_t"):
            self.eps_t = self.sb(self.es, "eps_t", [128, 1], F32)
            self.one_t = self.sb(self.es, "one_t", [128, 1], F32)
            kb.V(lambda e: e.memset(self.eps_t[:], EPS), writes=[Rc])
            kb.V(lambda e: e.memset(self.one_t[:], 1.0), writes=[Rc])
        x = self.sb(gs, "x" + G, [128, NCH, T], F32)
        xR = [Res("x%s_%d" % (G, b)) for b in range(nblk)]
        src = d["xs"] if isS else d["xp"]
        for b in range(nblk):
            kb.dma("sync", x[:, :, b * TB:(b + 1) * TB],
                   src.rearrange("(c p) t -> p c t", p=128)[:, :, b * TB:(b + 1) * TB], writes=[xR[b]])
        if not isS:
            self.ho = self.sb(gs, "ho", [128, L * 2 * NCH * cfg.np_seq], F32)
            self.hoR = Res("ho")
        else:
            self.car = self.sb(gs, "car", [128, NCH, 2, nblk], F32)
            self.carR = Res("car")
            self.hprev = self.sb(gs, "hprev", [128, NCH, 128], BF16)
            self.hprevR = Res("hprev")
        for l in range(L):
            if isS:
                self.lru_pass1(l, x, xR, T)
            for b in range(nblk):
                self.mixer_block(l, G, x, xR, T, b)
                if (not isS) and l == 0 and b == 0:
                    for l2 in range(1, L):
                        self.compute_mod(l2)
            self.ffn_phase(l, G, x, xR, T)
        with contextlib.ExitStack() as sc:
            pools = {}
            ypool = Pool(kb, sc, "yo", [128, NCH, 512], F32, 2)
            dst = d["ys"] if isS else d["yp"]
            fo = VEC_OFF["final_norm"][0]
            ydst = dst.rearrange("(c p) t -> p c t", p=128)
            self._final_norm(x, xR, T, self.vec[:, 0, fo:fo + 8], ypool, ydst)
            if not isS:
                kb.dma("sync", d["ho"][:], self.ho[:], reads=[self.hoR], is_out=True)
            kb.retire()

    def lru_pass1(self, l, x, xR, T):
        cfg, kb, nc, d = self.cfg, self.kb, self.nc, self.d
        Rc = self.R_c
        nblk = T // TB
        par = self.par
        T0 = TB - 2
        TW = T - TB
        with contextlib.ExitStack() as sc:
            pools = {"f512": Pool(kb, sc, "p1f", [128, 512], F32, 8)}
            h = self.sb(sc, "p1h", [128, NCH, T - T0], BF16)
            hR = Res("p1h")
            self.norm_cols(x, list(xR), T0, T, par[:, l, 1, 0, :], par[:, l, 1, 1, :],
                           lambda c, off, w: h[:, c, off:off + w], hR, pools)
            xa_pool = Pool(kb, sc, "p1xa", [128, 1, TW + 3], F32, 2)
            xcv_pool = Pool(kb, sc, "p1xcv", [128, TW], F32, 2)
            xcb_pool = Pool(kb, sc, "p1xcb", [128, TW], BF16, 2)
            lwp = Pool(kb, sc, "p1lw", [128, 512], BF16, 2)
            scp = Pool(kb, sc, "p1sc", [128, 512], F32, 3)
            wts = {}

            def front(c):
                st, cc = c // 4, c % 4
                if st not in wts:
                    wt, wr = self.wload(d["w_in"][l, st], 4096)
                    wts[st] = (wt[:, 0:4096].rearrange("p (k n) -> p k n", k=NCH), wr)
                w3, wr = wts[st]
                lwt, lwr = lwp.next()
                kb.dma("gpsimd", lwt[:], d["w_lru"][l, c], writes=[lwr])
                lw = lwt[:].rearrange("p (d g n) -> p d g n", d=2, g=2)
                xa, xaR = xa_pool.next()
                kb.V(lambda e: e.memset(xa[:, 0, TW + 2:TW + 3], 0.0), writes=[xaR])
                lhs = w3[:, :, cc * 128:(cc + 1) * 128]
                pt, pr = self.pacc.next()
                for kc in range(NCH):
                    kb.mm(pt[:, 0:2], lhs[:, kc, :], h[:, kc, 0:2], start=(kc == 0), stop=(kc == NCH - 1),
                          reads=[wr, hR], writes=[pr])
                kb.V(lambda e, pt=pt: e.tensor_copy(out=xa[:, 0, 0:2], in_=pt[:, 0:2]), reads=[pr], writes=[xaR])
                for b in range(1, nblk):
                    pt, pr = self.pacc.next()
                    c0 = b * TB - T0
                    for kc in range(NCH):
                        kb.mm(pt[:], lhs[:, kc, :], h[:, kc, c0:c0 + TB], start=(kc == 0), stop=(kc == NCH - 1),
                              reads=[wr, hR], writes=[pr])
                    kb.V(lambda e, b=b, pt=pt: e.tensor_copy(out=xa[:, 0, 2 + (b - 1) * TB:2 + b * TB], in_=pt[:]),
                         reads=[pr], writes=[xaR])
                xcv, xcR = xcv_pool.next()
                xcb, _ = xcb_pool.next()
                self.conv_lru(l, c, xa, xaR, 1, TW, xcv, xcb, xcR)
                return lw, lwr, xcv, xcb, xcR

            def body(c, fr):
                lw, lwr, xcv, xcb, xcR = fr
                order = list(range(nblk - 1, 0, -1))
                prev = None
                pcs = {}
                for oi, b in enumerate(order):
                    if oi % 2 == 0:
                        grp = order[oi:oi + 2]
                        rs = interleave([self.lru_piece(l, c, 1, lw, lwr, xcv, xcb, xcR, (bb - 1) * TB, TB, pools)
                                         for bb in grp])
                        for bb, r_ in zip(grp, rs):
                            pcs[bb] = r_
                    at, ar, bt, br = pcs.pop(b)
                    ht, hr = scp.next()
                    if prev is None:
                        init = self.h0s[:, l, 1, c:c + 1]
                        ir = Rc
                    else:
                        init = prev[0][:, 0:1]
                        ir = prev[1]
                    kb.V(lambda e, ht=ht, at=at, bt=bt, init=init: e.tensor_tensor_scan(
                        out=ht[:, ::-1], data0=at[:, ::-1], data1=bt[:, ::-1], initial=init,
                        op0=ALU.mult, op1=ALU.add), reads=[ar, br, ir], writes=[hr])
                    kb.A(lambda e, ht=ht, b=b: e.activation(out=self.car[:, c, 1, b:b + 1], in_=ht[:, 0:1], func=AF.Copy),
                         reads=[hr], writes=[self.carR])
                    prev = (ht, hr)

            fr = front(0)
            for c in range(NCH):
                fr_next = front(c + 1) if c + 1 < NCH else None
                body(c, fr)
                fr = fr_next
            kb.retire()

    def mixer_block(self, l, G, x, xR, T, b):
        cfg, kb, nc, d = self.cfg, self.kb, self.nc, self.d
        Rc = self.R_c
        isS = (G == "S")
        g = 1 if isS else 0
        par = self.par
        s, e_ = b * TB, (b + 1) * TB
        nblk = T // TB
        if isS:
            lo, hi = max(0, s - 128), min(T, e_ + 128)
            nseg, Lb = 1, TB
        else:
            lo, hi = s, e_
            nseg, Lb = TB // cfg.lp, cfg.lp
        W = hi - lo
        off = s - lo
        xdeps = [xR[bb] for bb in range(nblk) if bb * TB < hi and (bb + 1) * TB > lo]
        with contextlib.ExitStack() as sc:
            hb = self.sb(sc, "hb", [128, NCH, W], BF16)
            hR = Res("hb")
            ya = self.sb(sc, "ya", [128, NCH, TB], BF16)
            yaR = Res("ya")
            yb = self.sb(sc, "yb", [128, NCH, TB], BF16)
            ybR = Res("yb")
            yc = self.sb(sc, "yc", [128, NCH, TB], BF16)
            ycR = Res("yc")
            kT = self.sb(sc, "kT", [128, 2, W], BF16)
            kTR = Res("kT")
            vt = self.sb(sc, "vt", [128, W // 128, 256], BF16)
            vtR = Res("vt")
            f512 = Pool(kb, sc, "mf", [128, 512], F32, 8)
            with contextlib.ExitStack() as s1:
                pools = {"f512": f512}
                self._norm_multi(x, xdeps, s, hi, par[:, l, g, 0, :], par[:, l, g, 1, :],
                                 lambda c, o, w: hb[:, c, off + o:off + o + w], hR, pools)
                if off > 0:
                    kb.A(lambda e: e.activation(out=hb[:, :, 0:off], in_=self.hprev[:], func=AF.Copy),
                         reads=[self.hprevR], writes=[hR])
                kb.retire()
            with contextlib.ExitStack() as s2:
                pools = {"f512": f512}
                xa_pool = Pool(kb, s2, "mxa", [128, nseg, Lb + 3], F32, 2)
                xcv_pool = Pool(kb, s2, "mxcv", [128, TB], F32, 2)
                xcb_pool = Pool(kb, s2, "mxcb", [128, TB], BF16, 2)
                lwp = Pool(kb, s2, "mlw", [128, 512], BF16, 2)
                hfp = Pool(kb, s2, "mhf", [128, TB], F32, 2)
                hbp = Pool(kb, s2, "mhb", [128, TB], F32, 2)
                wts = {}

                def front(c):
                    st, cc = c // 4, c % 4
                    if st not in wts:
                        wt, wr = self.wload(d["w_in"][l, st], 4096)
                        wts[st] = (wt[:, 0:4096].rearrange("p (k n) -> p k n", k=NCH), wr)
                    w3, wr = wts[st]
                    lwt, lwr = lwp.next()
                    kb.dma("gpsimd", lwt[:], d["w_lru"][l, c], writes=[lwr])
                    lw = lwt[:].rearrange("p (d g n) -> p d g n", d=2, g=2)
                    xa, xaR = xa_pool.next()
                    lhs = w3[:, :, cc * 128:(cc + 1) * 128]
                    if isS:
                        c0 = max(0, s - 2)
                        c1 = min(T, e_ + 1)
                        if s == 0:
                            kb.V(lambda e: e.memset(xa[:, 0, 0:2], 0.0), writes=[xaR])
                        if e_ == T:
                            kb.V(lambda e: e.memset(xa[:, 0, Lb + 2:Lb + 3], 0.0), writes=[xaR])
                        pos = c0
                        while pos < c1:
                            w = min(512, c1 - pos)
                            pt, pr = self.pacc.next()
                            for kc in range(NCH):
                                kb.mm(pt[:, 0:w], lhs[:, kc, :], hb[:, kc, pos - lo:pos - lo + w],
                                      start=(kc == 0), stop=(kc == NCH - 1), reads=[wr, hR], writes=[pr])
                            dc = pos - (s - 2)
                            kb.V(lambda e, pt=pt, dc=dc, w=w: e.tensor_copy(out=xa[:, 0, dc:dc + w], in_=pt[:, 0:w]),
                                 reads=[pr], writes=[xaR])
                            pos += w
                    else:
                        kb.V(lambda e: e.memset(xa[:, :, 0:2], 0.0), writes=[xaR])
                        kb.V(lambda e: e.memset(xa[:, :, Lb + 2:Lb + 3], 0.0), writes=[xaR])
                        pt, pr = self.pacc.next()
                        for kc in range(NCH):
                            kb.mm(pt[:], lhs[:, kc, :], hb[:, kc, :], start=(kc == 0), stop=(kc == NCH - 1),
                                  reads=[wr, hR], writes=[pr])
                        kb.V(lambda e, pt=pt: e.tensor_copy(out=xa[:, :, 2:2 + Lb],
                                                            in_=pt[:].rearrange("p (s t) -> p s t", s=nseg)),
                             reads=[pr], writes=[xaR])
                    xcv, xcvR = xcv_pool.next()
                    xcb, _ = xcb_pool.next()
                    self.conv_lru(l, c, xa, xaR, nseg, Lb, xcv, xcb, xcvR)
                    return lw, lwr, xcv, xcb, xcvR

                def body(c, fr):
                    lw, lwr, xcv, xcb, xcvR = fr
                    hf, hfR = hfp.next()
                    hbk, hbR = hbp.next()
                    pres = interleave([self.lru_piece(l, c, dr, lw, lwr, xcv, xcb, xcvR, 0, TB, pools)
                                       for dr in range(2)])
                    return hf, hfR, hbk, hbR, pres

                def back(c, bd):
                    hf, hfR, hbk, hbR, pres = bd
                    for dr in range(2):
                        at, ar, bt, br = pres[dr]
                        for sg in range(nseg):
                            q0, q1 = sg * Lb, (sg + 1) * Lb
                            if isS:
                                if dr == 0:
                                    init = self.h0s[:, l, 0, c:c + 1] if b == 0 else self.car[:, c, 0, b - 1:b]
                                else:
                                    init = self.h0s[:, l, 1, c:c + 1] if b == nblk - 1 else self.car[:, c, 1, b + 1:b + 2]
                                ideps = [Rc, self.carR]
                            else:
                                init = 0.0
                                ideps = []
                            if dr == 0:
                                kb.V(lambda e, init=init, q0=q0, q1=q1: e.tensor_tensor_scan(
                                    out=hf[:, q0:q1], data0=at[:, q0:q1], data1=bt[:, q0:q1], initial=init,
                                    op0=ALU.mult, op1=ALU.add), reads=[ar, br] + ideps, writes=[hfR])
                            else:
                                kb.V(lambda e, init=init, q0=q0, q1=q1: e.tensor_tensor_scan(
                                    out=hbk[:, q0:q1][:, ::-1], data0=at[:, q0:q1][:, ::-1],
                                    data1=bt[:, q0:q1][:, ::-1], initial=init,
                                    op0=ALU.mult, op1=ALU.add), reads=[ar, br] + ideps, writes=[hbR])
                    kb.V(lambda e: e.tensor_tensor(out=ya[:, c, :], in0=hf[:], in1=hbk[:], op=ALU.add),
                         reads=[hfR, hbR], writes=[yaR])
                    if isS and b < nblk - 1:
                        kb.V(lambda e: e.tensor_copy(out=self.car[:, c, 0, b:b + 1], in_=hf[:, TB - 1:TB]),
                             reads=[hfR], writes=[self.carR])
                    if not isS:
                        for sg in range(nseg):
                            seq = b * nseg + sg
                            i0 = ((l * 2 + 0) * NCH + c) * cfg.np_seq + seq
                            i1 = ((l * 2 + 1) * NCH + c) * cfg.np_seq + seq
                            kb.A(lambda e, i0=i0, sg=sg: e.activation(out=self.ho[:, i0:i0 + 1],
                                                                      in_=hf[:, (sg + 1) * Lb - 1:(sg + 1) * Lb],
                                                                      func=AF.Copy), reads=[hfR], writes=[self.hoR])
                            kb.A(lambda e, i1=i1, sg=sg: e.activation(out=self.ho[:, i1:i1 + 1],
                                                                      in_=hbk[:, sg * Lb:sg * Lb + 1],
                                                                      func=AF.Copy), reads=[hbR], writes=[self.hoR])

                fr = front(0)
                for c in range(NCH):
                    fr_next = front(c + 1) if c + 1 < NCH else None
                    bd = body(c, fr)
                    back(c, bd)
                    fr = fr_next
                kb.retire()
            with contextlib.ExitStack() as s3:
                wt, wr = self.wload(d["w_in"][l, 4], 4096)
                w3 = wt[:, 0:4096].rearrange("p (k n) -> p k n", k=NCH)
                if isS:
                    rope = self.sb(s3, "ropek", [128, 2, W], F32)
                    rR = Res("ropek")
                    kb.dma("sync", rope[:], d["rope"].rearrange("a p t -> p a t")[:, :, lo:hi], writes=[rR])
                    kf_pool = Pool(kb, s3, "kf", [128, 512], F32, 2)
                for kv in range(2):
                    pos = 0
                    while pos < W:
                        w = min(512, W - pos)
                        pt, pr = self.psum.next()
                        for kc in range(NCH):
                            kb.mm(pt[:, 0:w], w3[:, kc, kv * 128:(kv + 1) * 128], hb[:, kc, pos:pos + w],
                                  start=(kc == 0), stop=(kc == NCH - 1), reads=[wr, hR], writes=[pr])
                        if isS:
                            kf, kfr = kf_pool.next()
                            kb.A(lambda e, pt=pt, w=w, kf=kf: e.activation(out=kf[:, 0:w], in_=pt[:, 0:w], func=AF.Copy),
                                 reads=[pr], writes=[kfr])
                            self.rope_apply(kf, kfr, w, rope[:, 0, pos:pos + w], rope[:, 1, pos:pos + w], rR,
                                            kT[:, kv, pos:pos + w], kTR, f512)
                        else:
                            kb.A(lambda e, pt=pt, w=w, kv=kv, pos=pos: e.activation(out=kT[:, kv, pos:pos + w],
                                                                                    in_=pt[:, 0:w], func=AF.Copy),
                                 reads=[pr], writes=[kTR])
                        pos += w
                if not isS:
                    kvo_pool = Pool(kb, s3, "kvo", [128, 512], F32, 2)
                for ch in range(W // 128):
                    pt, pr = self.psum.next()
                    if isS:
                        for kc in range(NCH):
                            kb.mm(pt[:, 0:256], hb[:, kc, ch * 128:(ch + 1) * 128], w3[:, kc, 256:512],
                                  start=(kc == 0), stop=(kc == NCH - 1), reads=[wr, hR], writes=[pr])
                        kb.A(lambda e, pt=pt, ch=ch: e.activation(out=vt[:, ch, :], in_=pt[:, 0:256], func=AF.Copy),
                             reads=[pr], writes=[vtR])
                    else:
                        for kc in range(NCH):
                            kb.mm(pt[:], hb[:, kc, ch * 128:(ch + 1) * 128], w3[:, kc, :],
                                  start=(kc == 0), stop=(kc == NCH - 1), reads=[wr, hR], writes=[pr])
                        ko, kor = kvo_pool.next()
                        kb.A(lambda e, pt=pt, ko=ko: e.activation(out=ko[:], in_=pt[:], func=AF.Copy),
                             reads=[pr], writes=[kor])
                        kb.V(lambda e, ko=ko, ch=ch: e.tensor_copy(out=vt[:, ch, :], in_=ko[:, 256:512]),
                             reads=[kor], writes=[vtR])
                        t0 = s + ch * 128
                        kb.dma("sync", d["ko"][l, t0:t0 + 128, :], ko[:, 0:256], reads=[kor], is_out=True)
                        kb.dma("sync", d["vo"][l, t0:t0 + 128, :], ko[:, 256:512], reads=[kor], is_out=True)
                kb.retire()
            with contextlib.ExitStack() as s4:
                if isS:
                    kc_t = self.sb(s4, "kcT", [128, 2, cfg.past], BF16)
                    vc_t = self.sb(s4, "vcT", [128, cfg.past // 128, 256], BF16)
                    cR = Res("ctx")
                    kb.dma("gpsimd", kc_t[:].rearrange("p a t -> p (a t)"), d["kcT"][l], writes=[cR])
                    kb.dma("gpsimd", vc_t[:].rearrange("p a t -> p (a t)"), d["vc"][l], writes=[cR])
                    ropeq = self.sb(s4, "ropeq", [128, 2, TB], F32)
                    rqR = Res("ropeq")
                    kb.dma("sync", ropeq[:], d["rope"].rearrange("a p t -> p a t")[:, :, s:e_], writes=[rqR])
                    qf_pool = Pool(kb, s4, "qf", [128, 512], F32, 2)
                qg_pool = Pool(kb, s4, "qg", [128, 4, TB], BF16, 2)
                pT_pool = Pool(kb, s4, "pT", [128, 512], BF16, 3)
                for kvg in range(2):
                    wt, wr = self.wload(d["w_in"][l, 2 + kvg], 4096)
                    w3 = wt[:, 0:4096].rearrange("p (k n) -> p k n", k=NCH)
                    qg, qgR = qg_pool.next()
                    for hh in range(4):
                        pt, pr = self.psum.next()
                        for kc in range(NCH):
                            kb.mm(pt[:], w3[:, kc, hh * 128:(hh + 1) * 128], hb[:, kc, off:off + TB],
                                  start=(kc == 0), stop=(kc == NCH - 1), reads=[wr, hR], writes=[pr])
                        if isS:
                            qf, qfr = qf_pool.next()
                            kb.V(lambda e, pt=pt, qf=qf: e.tensor_copy(out=qf[:], in_=pt[:]), reads=[pr], writes=[qfr])
                            self.rope_apply(qf, qfr, TB, ropeq[:, 0, :], ropeq[:, 1, :], rqR, qg[:, hh, :], qgR, f512)
                        else:
                            kb.V(lambda e, pt=pt, hh=hh: e.tensor_copy(out=qg[:, hh, :], in_=pt[:]), reads=[pr], writes=[qgR])
                    for j in range(TB // 128):
                        jq = (s // 128) + j
                        chunks = []
                        if isS:
                            for dj, mk in ((-1, 0), (0, None), (1, 1)):
                                jj = jq + dj
                                if jj < 0 or jj >= T // 128:
                                    continue
                                col = jj * 128 - lo
                                chunks.append((kT[:, kvg, col:col + 128], vt[:, col // 128, kvg * 128:(kvg + 1) * 128],
                                               mk, [kTR, vtR]))
                            for cj in range(cfg.past // 128):
                                chunks.append((kc_t[:, kvg, cj * 128:(cj + 1) * 128],
                                               vc_t[:, cj, kvg * 128:(kvg + 1) * 128], None, [cR]))
                        else:
                            sq0 = (jq * 128 // cfg.lp) * cfg.lp
                            for jj in range(cfg.lp // 128):
                                col = sq0 + jj * 128 - lo
                                chunks.append((kT[:, kvg, col:col + 128], vt[:, col // 128, kvg * 128:(kvg + 1) * 128],
                                               None, [kTR, vtR]))
                        po, por = self.pacc.next()
                        pd, pdr = self.pacc.next()
                        rhs_q = qg[:, :, j * 128:(j + 1) * 128]
                        n = len(chunks)
                        qk = {}

                        def issue_qk(ci):
                            kl, vl, mk, deps = chunks[ci]
                            ps_, psr = self.psum.next()
                            kb.mm(ps_[:].rearrange("p (a b) -> p a b", a=4), kl, rhs_q, True, True,
                                  reads=deps + [qgR], writes=[psr])
                            qk[ci] = (ps_, psr)

                        issue_qk(0)
                        if n > 1:
                            issue_qk(1)
                        for ci, (kl, vl, mk, deps) in enumerate(chunks):
                            ps_, psr = qk.pop(ci)
                            pT, pTr = pT_pool.next()
                            kb.A(lambda e, ps_=ps_, pT=pT: e.activation(out=pT[:], in_=ps_[:], func=AF.Exp,
                                                                        scale=float(128 ** -0.5)),
                                 reads=[psr], writes=[pTr])
                            if mk is not None:
                                kb.V(lambda e, pT=pT, mk=mk: e.tensor_tensor(out=pT[:], in0=pT[:],
                                                                             in1=self.masks[:, mk * 512:(mk + 1) * 512],
                                                                             op=ALU.mult), reads=[pTr, Rc], writes=[pTr])
                            if ci + 2 < n:
                                issue_qk(ci + 2)
                            kb.mm(po[:], vl, pT[:], start=(ci == 0), stop=(ci == n - 1), reads=deps + [pTr],
                                  writes=[por], inc=False)
                            kb.mm(pd[:], self.ones_1[:], pT[:], start=(ci == 0), stop=(ci == n - 1), reads=[pTr, Rc],
                                  writes=[pdr], inc=True)
                        rd, rdr = f512.next()
                        for hh in range(4):
                            hd = kvg * 4 + hh
                            kb.V(lambda e, hh=hh, hd=hd: e.tensor_scalar(out=rd[:, hh * 128:(hh + 1) * 128],
                                                                         in0=pd[:, hh * 128:(hh + 1) * 128],
                                                                         scalar1=self.esink[:, l, hd:hd + 1], scalar2=None,
                                                                         op0=ALU.add), reads=[pdr, Rc], writes=[rdr])
                        kb.A(lambda e: e.activation(out=rd[:], in_=rd[:], func=AF.Ln), reads=[rdr], writes=[rdr])
                        kb.A(lambda e: e.activation(out=rd[:], in_=rd[:], func=AF.Exp, scale=-1.0), reads=[rdr], writes=[rdr])
                        kb.V(lambda e, j=j: e.tensor_tensor(out=yb[:, kvg * 4:(kvg + 1) * 4, j * 128:(j + 1) * 128],
                                                            in0=po[:].rearrange("p (a b) -> p a b", a=4),
                                                            in1=rd[:].rearrange("p (a b) -> p a b", a=4), op=ALU.mult),
                             reads=[por, rdr], writes=[ybR])
                kb.retire()
            with contextlib.ExitStack() as s5:
                PW = Lb + 16
                xc_pool = Pool(kb, s5, "xc", [128, nseg, PW], F32, 2)
                sA_pool = Pool(kb, s5, "sA", [128, nseg, PW], F32, 2)
                sB_pool = Pool(kb, s5, "sB", [128, nseg, PW], F32, 2)
                pl_pool = Pool(kb, s5, "pl", [128, 2, TB], BF16, 2)
                pwp = Pool(kb, s5, "pw", [128, 512], BF16, 2)
                pws = {}

                def pfront(gi):
                        win = 2 << gi
                        st = 5 + gi // 2
                        if st not in pws:
                            wt_, wr_ = self.wload(d["w_in"][l, st], 4096)
                            pws[st] = (wt_[:, 0:4096].rearrange("p (k n) -> p k n", k=NCH), wr_)
                        w3, wr = pws[st]
                        pwt, pwr = pwp.next()
                        kb.dma("gpsimd", pwt[:], d["w_pool"][l, gi], writes=[pwr])
                        pw3 = pwt[:].rearrange("p (i n) -> p i n", i=2)
                        pl, plR = pl_pool.next()
                        for ic in range(2):
                            cc = (gi % 2) * 2 + ic
                            xc, xcR = xc_pool.next()
                            lhs = w3[:, :, cc * 128:(cc + 1) * 128]
                            if isS:
                                c0 = max(0, s - 8)
                                c1 = min(T, e_ + 8)
                                if s == 0:
                                    kb.V(lambda e, xc=xc: e.memset(xc[:, 0, 0:8], 0.0), writes=[xcR])
                                if e_ == T:
                                    kb.V(lambda e, xc=xc: e.memset(xc[:, 0, Lb + 8:Lb + 16], 0.0), writes=[xcR])
                                pos = c0
                                while pos < c1:
                                    w = min(512, c1 - pos)
                                    pt, pr = self.psum.next()
                                    for kc in range(NCH):
                                        kb.mm(pt[:, 0:w], lhs[:, kc, :], hb[:, kc, pos - lo:pos - lo + w],
                                              start=(kc == 0), stop=(kc == NCH - 1), reads=[wr, hR], writes=[pr])
                                    dc = pos - (s - 8)
                                    kb.A(lambda e, pt=pt, dc=dc, w=w, xc=xc: e.activation(out=xc[:, 0, dc:dc + w],
                                                                                          in_=pt[:, 0:w], func=AF.Copy),
                                         reads=[pr], writes=[xcR])
                                    pos += w
                            else:
                                kb.V(lambda e, xc=xc: e.memset(xc[:, :, 0:8], 0.0), writes=[xcR])
                                kb.V(lambda e, xc=xc: e.memset(xc[:, :, Lb + 8:Lb + 16], 0.0), writes=[xcR])
                                pt, pr = self.psum.next()
                                for kc in range(NCH):
                                    kb.mm(pt[:], lhs[:, kc, :], hb[:, kc, :], start=(kc == 0), stop=(kc == NCH - 1),
                                          reads=[wr, hR], writes=[pr])
                                kb.A(lambda e, pt=pt, xc=xc: e.activation(out=xc[:, :, 8:8 + Lb],
                                                                          in_=pt[:].rearrange("p (s t) -> p s t", s=nseg),
                                                                          func=AF.Copy), reads=[pr], writes=[xcR])
                            cur, curR = xc, xcR
                            lo_u, hi_u = 0, PW
                            sh = 0
                            step = 0
                            while (1 << step) < win:
                                nxt, nxtR = (sA_pool if step % 2 == 0 else sB_pool).next()
                                if step == 0:
                                    a0, a1 = lo_u + 1, hi_u
                                    kb.V(lambda e, cur=cur, nxt=nxt, a0=a0, a1=a1: e.tensor_tensor(
                                        out=nxt[:, :, a0:a1], in0=cur[:, :, a0 - 1:a1 - 1], in1=cur[:, :, a0:a1], op=ALU.add),
                                        reads=[curR], writes=[nxtR])
                                    lo_u, hi_u = a0, a1
                                else:
                                    dlt = 1 << (step - 1)
                                    a0, a1 = lo_u + dlt, hi_u - dlt
                                    kb.V(lambda e, cur=cur, nxt=nxt, a0=a0, a1=a1, dlt=dlt: e.tensor_tensor(
                                        out=nxt[:, :, a0:a1], in0=cur[:, :, a0 - dlt:a1 - dlt], in1=cur[:, :, a0 + dlt:a1 + dlt],
                                        op=ALU.add), reads=[curR], writes=[nxtR])
                                    lo_u, hi_u = a0, a1
                                cur, curR = nxt, nxtR
                                step += 1
                            m = cur
                            kb.V(lambda e, m=m: e.tensor_scalar(out=m[:, :, 8:8 + Lb], in0=m[:, :, 8:8 + Lb], scalar1=1.0 / win,
                                                                scalar2=None, op0=ALU.mult), reads=[curR], writes=[curR])
                            segs_l = range(nseg) if not isS else ([0] if s == 0 else [])
                            segs_r = range(nseg) if not isS else ([0] if e_ == T else [])
                            for sg in segs_l:
                                kb.V(lambda e, m=m, sg=sg: e.tensor_tensor(out=m[:, sg, 8:16], in0=m[:, sg, 8:16],
                                                                           in1=self.pedge[:, gi * 16:gi * 16 + 8], op=ALU.mult),
                                     reads=[curR, Rc], writes=[curR])
                            for sg in segs_r:
                                kb.V(lambda e, m=m, sg=sg: e.tensor_tensor(out=m[:, sg, Lb:Lb + 8], in0=m[:, sg, Lb:Lb + 8],
                                                                           in1=self.pedge[:, gi * 16 + 8:gi * 16 + 16],
                                                                           op=ALU.mult), reads=[curR, Rc], writes=[curR])
                            kb.V(lambda e, m=m, xc=xc, ic=ic: e.tensor_tensor(
                                out=pl[:, ic, :].rearrange("p (s t) -> p s t", s=nseg), in0=m[:, :, 8:8 + Lb],
                                in1=xc[:, :, 8:8 + Lb], op=ALU.subtract), reads=[curR, xcR], writes=[plR])
                        return pw3, pwr, pl, plR

                def pback(gi, fr_):
                        pw3, pwr, pl, plR = fr_
                        for jc in range(2):
                            pt, pr = self.pacc.next()
                            for ic in range(2):
                                kb.mm(pt[:], pw3[:, ic, jc * 128:(jc + 1) * 128], pl[:, ic, :], start=(ic == 0), stop=(ic == 1),
                                      reads=[pwr, plR], writes=[pr])
                            oc = gi * 2 + jc
                            kb.A(lambda e, pt=pt, oc=oc: e.activation(out=yc[:, oc, :], in_=pt[:], func=AF.Identity,
                                                                      scale=self.vcol(l, "pool_scale", oc)),
                                 reads=[pr, Rc], writes=[ycR])

                pfr = {0: pfront(0)}
                for gi in range(4):
                    if gi + 1 < 4:
                        pfr[gi + 1] = pfront(gi + 1)
                    pback(gi, pfr.pop(gi))
                kb.retire()
            with contextlib.ExitStack() as s6:
                mg = self.sb(s6, "mg", [128, NCH, TB], BF16)
                mgR = Res("mg")
                gp = Pool(kb, s6, "gt", [128, TB], F32, 3)
                ap_ = Pool(kb, s6, "acc", [128, TB], F32, 2)
                tp_ = Pool(kb, s6, "tmp", [128, TB], F32, 2)
                ys = [(ya, yaR), (yb, ybR), (yc, ycR)]
                for m in range(NCH):
                    wg, wgr = self.wload(d["w_g"][l, m], 3072)
                    wg4 = wg[:, 0:3072].rearrange("p (b k n) -> p b k n", b=3, k=NCH)
                    wbt, wbr = self.wload(d["w_br"][l, m], 3072)
                    wb4 = wbt[:, 0:3072].rearrange("p (b k n) -> p b k n", b=3, k=NCH)
                    acc, accR = ap_.next()
                    for br_ in range(3):
                        pg, pgr = self.psum.next()
                        for kc in range(NCH):
                            kb.mm(pg[:], wg4[:, br_, kc, :], hb[:, kc, off:off + TB], start=(kc == 0),
                                  stop=(kc == NCH - 1), reads=[wgr, hR], writes=[pgr])
                        gt, gtr = gp.next()
                        bo = br_ * 8 + m
                        kb.A(lambda e, pg=pg, gt=gt, bo=bo: e.activation(out=gt[:], in_=pg[:], func=AF.Tanh, scale=0.5,
                                                                         bias=self.bgh[:, l, bo:bo + 1]),
                             reads=[pgr, Rc], writes=[gtr])
                        pb, pbr = self.psum.next()
                        yt, ytR = ys[br_]
                        for kc in range(NCH):
                            kb.mm(pb[:], wb4[:, br_, kc, :], yt[:, kc, :], start=(kc == 0), stop=(kc == NCH - 1),
                                  reads=[wbr, ytR], writes=[pbr])
                        if br_ == 0:
                            kb.V(lambda e, gt=gt, pb=pb, acc=acc: e.scalar_tensor_tensor(out=acc[:], in0=gt[:], scalar=1.0,
                                                                                         in1=pb[:], op0=ALU.add,
                                                                                         op1=ALU.mult),
                                 reads=[gtr, pbr], writes=[accR])
                        else:
                            tm, tmr = tp_.next()
                            kb.V(lambda e, gt=gt, pb=pb, tm=tm: e.scalar_tensor_tensor(out=tm[:], in0=gt[:], scalar=1.0,
                                                                                       in1=pb[:], op0=ALU.add,
                                                                                       op1=ALU.mult),
                                 reads=[gtr, pbr], writes=[tmr])
                            if br_ == 1:
                                kb.V(lambda e, tm=tm, acc=acc: e.tensor_tensor(out=acc[:], in0=acc[:], in1=tm[:], op=ALU.add),
                                     reads=[accR, tmr], writes=[accR])
                            else:
                                kb.V(lambda e, tm=tm, acc=acc, m=m: e.tensor_tensor(out=mg[:, m, :], in0=acc[:], in1=tm[:],
                                                                                    op=ALU.add),
                                     reads=[accR, tmr], writes=[mgR])
                for st in range(2):
                    wt, wr = self.wload(d["w_out"][l, st], 4096)
                    w3 = wt[:, 0:4096].rearrange("p (k n) -> p k n", k=NCH)
                    for cc in range(4):
                        m = st * 4 + cc
                        pt, pr = self.psum.next()
                        for kc in range(NCH):
                            kb.mm(pt[:], w3[:, kc, cc * 128:(cc + 1) * 128], mg[:, kc, :], start=(kc == 0),
                                  stop=(kc == NCH - 1), reads=[wr, mgR], writes=[pr])
                        kb.V(lambda e, pt=pt, m=m: e.scalar_tensor_tensor(out=x[:, m, s:e_], in0=pt[:],
                                                                          scalar=self.par[:, l, g, 2, m:m + 1],
                                                                          in1=x[:, m, s:e_], op0=ALU.mult, op1=ALU.add),
                             reads=[pr, Rc] + xdeps, writes=[xR[b]])
                if isS and b < nblk - 1:
                    kb.A(lambda e: e.activation(out=self.hprev[:], in_=hb[:, :, off + TB - 128:off + TB], func=AF.Copy),
                         reads=[hR], writes=[self.hprevR])
                if self.dbg == (G, l, b):
                    self.dump("hb", hb[:], [128, NCH, W], [hR])
                    self.dump("ya", ya[:], [128, NCH, TB], [yaR])
                    self.dump("yb", yb[:], [128, NCH, TB], [ybR])
                    self.dump("yc", yc[:], [128, NCH, TB], [ycR])
                    self.dump("mg", mg[:], [128, NCH, TB], [mgR])
                    self.dump("kT", kT[:], [128, 2, W], [kTR])
                    self.dump("xmid", x[:, :, s:e_], [128, NCH, TB], [xR[b]])
                kb.retire()

    def _norm_multi(self, x, xdeps, lo, hi, A, B, out_fn, outR, pools, done_fn=None):
        kb = self.kb
        Rc = self.R_c
        subs = []
        pos = lo
        while pos < hi:
            w = min(512, hi - pos)
            subs.append((pos, w))
            pos += w

        def stage_a(pos, w):
            sq, sqr = self.n_sq.next()
            kb.A(lambda e: e.activation(out=sq[:, :, 0:w], in_=x[:, :, pos:pos + w], func=AF.Square),
                 reads=xdeps, writes=[sqr])
            pt, pr = self.psum.next()
            for c in range(NCH):
                kb.mm(pt[:, 0:w], self.ones_m[:], sq[:, c, 0:w], start=(c == 0), stop=(c == NCH - 1),
                      reads=[sqr, Rc], writes=[pr])
            rt, rr = self.n_rs.next()
            kb.A(lambda e: e.activation(out=rt[:, 0:w], in_=pt[:, 0:w], func=AF.Ln, bias=EPS),
                 reads=[pr, Rc], writes=[rr])
            kb.A(lambda e: e.activation(out=rt[:, 0:w], in_=rt[:, 0:w], func=AF.Exp, scale=-0.5), reads=[rr], writes=[rr])
            return rt, rr

        def stage_b(pos, w, rt, rr):
            for c in range(NCH):
                o = out_fn(c, pos - lo, w)
                if B is None:
                    kb.V(lambda e: e.scalar_tensor_tensor(out=o, in0=x[:, c, pos:pos + w], scalar=A[:, c:c + 1],
                                                          in1=rt[:, 0:w], op0=ALU.mult, op1=ALU.mult),
                         reads=xdeps + [rr, Rc], writes=[outR])
                    continue
                tt, tr = self.n_tt.next()
                kb.V(lambda e: e.scalar_tensor_tensor(out=tt[:, 0:w], in0=x[:, c, pos:pos + w], scalar=A[:, c:c + 1],
                                                      in1=rt[:, 0:w], op0=ALU.mult, op1=ALU.mult),
                     reads=xdeps + [rr, Rc], writes=[tr])
                if c % 2 == 0:
                    kb.A(lambda e: e.activation(out=o, in_=tt[:, 0:w], func=AF.Identity, bias=B[:, c:c + 1]),
                         reads=[tr, Rc], writes=[outR])
                else:
                    kb.V(lambda e: e.tensor_scalar(out=o, in0=tt[:, 0:w], scalar1=B[:, c:c + 1], scalar2=None,
                                                   op0=ALU.add), reads=[tr, Rc], writes=[outR])
            if done_fn is not None:
                done_fn(pos, w)

        cur = stage_a(*subs[0])
        for k, (pos, w) in enumerate(subs):
            nxt = stage_a(*subs[k + 1]) if k + 1 < len(subs) else None
            stage_b(pos, w, *cur)
            cur = nxt

    def _final_norm(self, x, xR, T, A, ypool, ydst):
        kb = self.kb
        for b in range(T // TB):
            yt, yr = ypool.next()
            self._norm_multi(x, [xR[b]], b * TB, (b + 1) * TB, A, None,
                             lambda c, off, w, yt=yt: yt[:, c, off:off + w], yr, None,
                             lambda pos, w, yt=yt, yr=yr: kb.dma("sync", ydst[:, :, pos:pos + w], yt[:, :, 0:w],
                                                                 reads=[yr], is_out=True))

    def rope_apply(self, qf, qfr, w, cos, sin, rR, out, outR, f512):
        kb = self.kb
        Rc = self.R_c
        pp, ppr = self.psum.next()
        kb.mm(pp[:, 0:w], self.perm[:], qf[:, 0:w], True, True, reads=[Rc, qfr], writes=[ppr])
        t1, t1r = f512.next()
        kb.V(lambda e: e.tensor_tensor(out=t1[:, 0:w], in0=pp[:, 0:w], in1=sin, op=ALU.mult), reads=[ppr, rR], writes=[t1r])
        kb.V(lambda e: e.tensor_tensor(out=qf[:, 0:w], in0=qf[:, 0:w], in1=cos, op=ALU.mult), reads=[qfr, rR], writes=[qfr])
        kb.V(lambda e: e.tensor_tensor(out=out, in0=qf[:, 0:w], in1=t1[:, 0:w], op=ALU.add), reads=[qfr, t1r],
             writes=[outR])

    def ffn_phase(self, l, G, x, xR, T):
        cfg, kb, nc, d = self.cfg, self.kb, self.nc, self.d
        Rc = self.R_c
        isS = (G == "S")
        g = 1 if isS else 0
        par = self.par
        nblk = T // TB
        with contextlib.ExitStack() as sc:
            h2 = self.sb(sc, "h2", [128, NCH, T], BF16)
            hR = Res("h2")
            f512 = Pool(kb, sc, "ff", [128, 512], F32, 6)
            with contextlib.ExitStack() as s1:
                pools = {"f512": f512}
                self._norm_multi(x, list(xR), 0, T, par[:, l, g, 3, :], par[:, l, g, 4, :],
                                 lambda c, o, w: h2[:, c, o:o + w], hR, pools)
                kb.retire()
            if isS:
                nseg, Lb = 1, TB
            else:
                nseg, Lb = TB // cfg.lp, cfg.lp
            act = self.sb(sc, "act", [128, NFF, TB], BF16)
            actR = Res("act")
            u1p = Pool(kb, sc, "u1", [128, nseg, Lb + 2], F32, 4)
            gfp = Pool(kb, sc, "gf", [128, TB], F32, 4)
            u1_tiles = [u1p.next() for _ in range(4)]
            u1_i = [0]
            if not isS:
                for u1, u1R in u1_tiles:
                    kb.V(lambda e, u1=u1: e.memset(u1[:, :, 0:1], 0.0), writes=[u1R])
                    kb.V(lambda e, u1=u1: e.memset(u1[:, :, Lb + 1:Lb + 2], 0.0), writes=[u1R])

            def ffn_chunk(b, s, e_, w4, wr, c, cc):
                u1, u1R = u1_tiles[u1_i[0] % 4]
                u1_i[0] += 1
                lhs = w4[:, 0, :, cc * 128:(cc + 1) * 128]
                if isS:
                    c0 = max(0, s - 1)
                    c1 = min(T, e_ + 1)
                    if s == 0:
                        kb.V(lambda e: e.memset(u1[:, 0, 0:1], 0.0), writes=[u1R])
                    if e_ == T:
                        kb.V(lambda e: e.memset(u1[:, 0, Lb + 1:Lb + 2], 0.0), writes=[u1R])
                    pos = c0
                    while pos < c1:
                        w = min(512, c1 - pos)
                        pt, pr = self.psum.next()
                        for kc in range(NCH):
                            kb.mm(pt[:, 0:w], lhs[:, kc, :], h2[:, kc, pos:pos + w], start=(kc == 0),
                                  stop=(kc == NCH - 1), reads=[wr, hR], writes=[pr])
                        dc = pos - (s - 1)
                        kb.A(lambda e, pt=pt, dc=dc, w=w: e.activation(out=u1[:, 0, dc:dc + w], in_=pt[:, 0:w],
                                                                       func=AF.Copy), reads=[pr], writes=[u1R])
                        pos += w
                else:
                    pt, pr = self.psum.next()
                    for kc in range(NCH):
                        kb.mm(pt[:], lhs[:, kc, :], h2[:, kc, s:e_], start=(kc == 0), stop=(kc == NCH - 1),
                              reads=[wr, hR], writes=[pr])
                    kb.A(lambda e: e.activation(out=u1[:, :, 1:1 + Lb], in_=pt[:].rearrange("p (s t) -> p s t", s=nseg),
                                                func=AF.Copy), reads=[pr], writes=[u1R])
                p2, p2r = self.pacc.next()
                for kc in range(NCH):
                    kb.mm(p2[:], w4[:, 1, kc, cc * 128:(cc + 1) * 128], h2[:, kc, s:e_], start=(kc == 0),
                          stop=(kc == NCH - 1), reads=[wr, hR], writes=[p2r])
                yield
                gf, gfR = gfp.next()
                go = gf[:].rearrange("p (s t) -> p s t", s=nseg)
                kb.A(lambda e: e.activation(out=go, in_=u1[:, :, 0:Lb], func=AF.Identity,
                                            scale=self.vcol(l, "fconv0", c), bias=self.vcol(l, "fconv_b", c)),
                     reads=[u1R, Rc], writes=[gfR])
                yield
                for k in range(1, 3):
                    kb.V(lambda e, k=k: e.scalar_tensor_tensor(out=go, in0=u1[:, :, k:k + Lb],
                                                               scalar=self.vcol(l, "fconv%d" % k, c), in1=go,
                                                               op0=ALU.mult, op1=ALU.add), reads=[u1R, gfR, Rc], writes=[gfR])
                    yield
                kb.A(lambda e: e.activation(out=gf[:], in_=gf[:], func=AF.Gelu_apprx_tanh), reads=[gfR], writes=[gfR])
                yield
                kb.V(lambda e: e.tensor_tensor(out=act[:, c, :], in0=gf[:], in1=p2[:], op=ALU.mult),
                     reads=[gfR, p2r], writes=[actR])
                yield

            for b in range(nblk):
                s, e_ = b * TB, (b + 1) * TB
                for grp in range(11):
                    wt, wr = self.wload(d["w_up"][l, grp], 4096)
                    w4 = wt[:, 0:4096].rearrange("p (u k n) -> p u k n", u=2, k=NCH)
                    interleave([ffn_chunk(b, s, e_, w4, wr, grp * 2 + cc, cc) for cc in range(2)])
                for st in range(8):
                    wt, wr = self.wload(d["w_dn"][l, st], 2816)
                    w3 = wt[:, 0:2816].rearrange("p (k n) -> p k n", k=NFF)
                    for cc in range(1):
                        m = st
                        pt, pr = self.psum.next()
                        for kc in range(NFF):
                            kb.mm(pt[:], w3[:, kc, :], act[:, kc, :], start=(kc == 0),
                                  stop=(kc == NFF - 1), reads=[wr, actR], writes=[pr])
                        kb.V(lambda e, pt=pt, m=m: e.scalar_tensor_tensor(out=x[:, m, s:e_], in0=pt[:],
                                                                          scalar=self.par[:, l, g, 5, m:m + 1],
                                                                          in1=x[:, m, s:e_], op0=ALU.mult, op1=ALU.add),
                             reads=[pr, Rc, xR[b]], writes=[xR[b]])
                if self.dbg == (G, l, b):
                    self.dump("act", act[:], [128, NFF, TB], [actR])
                    self.dump("xffn", x[:, :, s:e_], [128, NCH, TB], [xR[b]])
                    kb.retire()
            kb.retire()


def _tile_w(W, cw):
    K, N = W.shape
    a = W.reshape(K // 128, 128, N // cw, cw)
    return np.ascontiguousarray(a.transpose(2, 1, 0, 3)).reshape(N // cw, 128, (K // 128) * cw)


def host_consts(cfg):
    TS = cfg.ts
    t = np.arange(TS)
    row = (t // cfg.grid_w).astype(np.float32)
    col = (t % cfg.grid_w).astype(np.float32)
    nf = 32
    freqs = (10000.0 ** (-np.arange(nf, dtype=np.float32) / nf)).astype(np.float32)
    cos = np.zeros((128, TS), np.float32)
    sin = np.zeros((128, TS), np.float32)
    ang_r = (row[None, :] * freqs[:, None]).astype(np.float32)
    ang_c = (col[None, :] * freqs[:, None]).astype(np.float32)
    cos[0:32] = np.cos(ang_r); cos[32:64] = np.cos(ang_r)
    cos[64:96] = np.cos(ang_c); cos[96:128] = np.cos(ang_c)
    sin[0:32] = -np.sin(ang_r); sin[32:64] = np.sin(ang_r)
    sin[64:96] = -np.sin(ang_c); sin[96:128] = np.sin(ang_c)
    rope = np.stack([cos, sin]).astype(np.float32)
    perm = np.zeros((128, 128), np.float32)
    for m in range(128):
        base = (m // 64) * 64
        i = m - base
        partner = base + (i + 32) % 64
        perm[partner, m] = 1.0
    kk = np.arange(128)[:, None]
    qq = np.arange(128)[None, :]
    mL = (qq <= kk).astype(np.float32)
    mR = (kk <= qq).astype(np.float32)
    masks = np.concatenate([np.tile(mL, (1, 4)), np.tile(mR, (1, 4))], axis=1).astype(np.float32)
    pedge = np.zeros((128, 64), np.float32)
    for gi in range(4):
        win = 2 << gi
        for j in range(8):
            cl = min(j + win // 2, win)
            cr = min(j + 1 + win // 2, win)
            pedge[:, gi * 16 + j] = win / cl
            pedge[:, gi * 16 + 8 + (7 - j)] = win / cr
    return rope, perm, masks, pedge


def prep_shared(inp, cfg):
    L = cfg.depth
    sh = {}
    vecs = np.zeros((L, 128, NV), np.float32)

    def put(l, name, v):
        o, w = VEC_OFF[name]
        vecs[l, :, o:o + w] = _fm(v)

    for l in range(L):
        put(l, "norm1", inp["norm1"][l]); put(l, "norm2", inp["norm2"][l]); put(l, "b_ada", inp["b_ada"][l])
        for k in range(4):
            put(l, "conv%d" % k, inp["lru_conv"][l, k])
        put(l, "conv_b", inp["lru_conv_b"][l])
        for dr in range(2):
            put(l, "ba%d" % dr, inp["lru_ba"][l, dr]); put(l, "bx%d" % dr, inp["lru_bx"][l, dr])
            put(l, "lam%d" % dr, inp["lru_lambda"][l, dr])
        put(l, "b_gate", inp["b_gate"][l]); put(l, "pool_scale", inp["pool_scale"][l])
        for k in range(3):
            put(l, "fconv%d" % k, inp["ffn_conv"][l, k])
        put(l, "fconv_b", inp["ffn_conv_b"][l])
        put(l, "final_norm", inp["final_norm"])
    sh["vecs"] = vecs
    sh["w_ada"] = np.stack([_tile_w(np.asarray(inp["w_ada"][l]), 512) for l in range(L)])
    sh["w_in"] = np.stack([_tile_w(np.asarray(inp["w_in"][l][:, 0:3584]), 512) for l in range(L)])
    wg = np.zeros((L, 8, 128, 3072), np.float32)
    wb = np.zeros((L, 8, 128, 3072), np.float32)
    for l in range(L):
        G = np.asarray(inp["w_in"][l][:, 3584:]).reshape(8, 128, 3, 8, 128)
        wg[l] = G.transpose(3, 1, 2, 0, 4).reshape(8, 128, 3072)
        Bm = np.asarray(inp["w_branch"][l]).reshape(3, 8, 128, 8, 128)
        wb[l] = Bm.transpose(3, 2, 0, 1, 4).reshape(8, 128, 3072)
    sh["w_g"] = wg
    sh["w_br"] = wb
    sh["w_out"] = np.stack([_tile_w(np.asarray(inp["w_out"][l]), 512) for l in range(L)])
    wu = np.zeros((L, 11, 128, 4096), np.float32)
    for l in range(L):
        U = np.asarray(inp["ffn_up"][l]).reshape(8, 128, 2, 11, 256)
        wu[l] = U.transpose(3, 1, 2, 0, 4).reshape(11, 128, 4096)
    sh["w_up"] = wu
    sh["w_dn"] = np.stack([_tile_w(np.asarray(inp["ffn_down"][l]), 128) for l in range(L)])
    wl = np.zeros((L, 8, 128, 512), np.float32)
    for l in range(L):
        A = np.stack([np.asarray(inp["lru_wa"][l]), np.asarray(inp["lru_wx"][l])], axis=0)
        wl[l] = A.transpose(2, 3, 1, 0, 4).reshape(8, 128, 512)
    sh["w_lru"] = wl
    wp = np.zeros((L, 4, 128, 512), np.float32)
    for l in range(L):
        Pw = np.asarray(inp["pool_w"][l]).reshape(4, 2, 128, 256)
        wp[l] = Pw.transpose(0, 2, 1, 3).reshape(4, 128, 512)
    sh["w_pool"] = wp
    sh["sink"] = np.ascontiguousarray(np.broadcast_to(np.asarray(inp["attn_sink"])[:, None, :], (L, 128, 8))).astype(np.float32)
    rope, perm, masks, pedge = host_consts(cfg)
    sh["rope"] = rope; sh["perm"] = perm; sh["masks"] = masks; sh["pedge"] = pedge
    return sh


def prep_core(inp, cfg, core, sh):
    L = cfg.depth
    npq = cfg.np_seq
    m = dict(sh)
    xp = np.asarray(inp["x_prompt"][core * npq:(core + 1) * npq]).reshape(cfg.tp, D)
    m["xp"] = np.ascontiguousarray(xp.T)
    n_s = inp["x_sample"].shape[0]
    sb = min(core // (8 // n_s), n_s - 1) if n_s <= 8 else 0
    m["xs"] = np.ascontiguousarray(np.asarray(inp["x_sample"][sb]).T)
    cv = np.zeros((128, NCH, 2), np.float32)
    cv[:, :, 0] = _fm(inp["c_ctx"])
    cv[:, :, 1] = _fm(inp["c"][sb])
    m["cvec"] = cv
    h0 = np.zeros((L, 128, 2, NCH), np.float32)
    for l in range(L):
        for dr in range(2):
            h0[l, :, dr, :] = _fm(inp["state_lru"][sb, l, dr])
    m["h0s"] = h0
    ck = np.asarray(inp["cache_k"][sb])
    m["kcT"] = np.ascontiguousarray(ck.transpose(0, 3, 2, 1)).reshape(L, 128, 2 * cfg.past)
    cvv = np.asarray(inp["cache_v"][sb]).reshape(L, cfg.past // 128, 128, 256)
    m["vc"] = np.ascontiguousarray(cvv.transpose(0, 2, 1, 3)).reshape(L, 128, (cfg.past // 128) * 256)
    return m, sb


_NC_CACHE = {}


def run(inp, cfg, ncores=8):
    key = (cfg.np_seq, cfg.lp, cfg.ts, cfg.past, cfg.grid_w, cfg.depth)
    if key not in _NC_CACHE:
        _NC_CACHE[key] = Builder(cfg).build()
    nc = _NC_CACHE[key]
    sh = prep_shared(inp, cfg)
    maps, sbs = [], []
    for core in range(ncores):
        m, sb = prep_core(inp, cfg, core, sh)
        maps.append(m)
        sbs.append(sb)
    res = run_bass_kernel_spmd(nc, maps, core_ids=list(range(ncores)))
    return res, sbs


def assemble(res, sbs, cfg, n_prompt, n_sample, ncores=8):
    L = cfg.depth
    npq = cfg.np_seq
    yp = np.zeros((n_prompt, cfg.lp, D), np.float32)
    ys = np.zeros((n_sample, cfg.ts, D), np.float32)
    nk = np.zeros((n_prompt, L, cfg.lp, 2, 128), np.float32)
    nv = np.zeros((n_prompt, L, cfg.lp, 2, 128), np.float32)
    nh = np.zeros((n_prompt, L, 2, D), np.float32)
    done = set()
    for core in range(ncores):
        r = res.results[core]
        yp[core * npq:(core + 1) * npq] = r["yp"].T.reshape(npq, cfg.lp, D)
        if sbs[core] not in done:
            ys[sbs[core]] = r["ys"].T
            done.add(sbs[core])
        nk[core * npq:(core + 1) * npq] = r["ko"].reshape(L, npq, cfg.lp, 2, 128).transpose(1, 0, 2, 3, 4)
        nv[core * npq:(core + 1) * npq] = r["vo"].reshape(L, npq, cfg.lp, 2, 128).transpose(1, 0, 2, 3, 4)
        ho = r["ho"].reshape(128, L, 2, NCH, npq)
        nh[core * npq:(core + 1) * npq] = ho.transpose(4, 1, 2, 3, 0).reshape(npq, L, 2, D)
    return yp, ys, nk, nv, nh


def kernel(**inputs):
    cfg = CFG()
    inp = {k: np.asarray(v) for k, v in inputs.items()}
    res, sbs = run(inp, cfg)
    return assemble(res, sbs, cfg, inp["x_prompt"].shape[0], inp["x_sample"].shape[0])
```

```python
import contextlib
import numpy as np
import concourse.bass as bass
import concourse.mybir as mybir
from concourse.bass_utils import run_bass_kernel_spmd

F32 = mybir.dt.float32
BF16 = mybir.dt.bfloat16
AF = mybir.ActivationFunctionType
ALU = mybir.AluOpType

D = 1024
NCH = 8
DEPTH = 2
D_IN = 6656
D_FF = 2816
NFF = 22
EPS = 1e-6
TB = 512
NSEM_DMA = 12
NWBUF = 4
WBUF_EL = 4096


class CFG:
    def __init__(self, np_seq=4, lp=256, ts=2048, past=512, grid_w=64, depth=DEPTH):
        self.np_seq = np_seq
        self.lp = lp
        self.tp = np_seq * lp
        self.ts = ts
        self.past = past
        self.grid_w = grid_w
        self.depth = depth


VEC_SPEC = [("norm1", 8), ("norm2", 8), ("b_ada", 48), ("conv0", 8), ("conv1", 8), ("conv2", 8), ("conv3", 8),
            ("conv_b", 8), ("ba0", 8), ("ba1", 8), ("bx0", 8), ("bx1", 8), ("lam0", 8), ("lam1", 8),
            ("b_gate", 24), ("pool_scale", 8), ("fconv0", 22), ("fconv1", 22), ("fconv2", 22), ("fconv_b", 22),
            ("final_norm", 8)]
VEC_OFF = {}
_o = 0
for _n, _w in VEC_SPEC:
    VEC_OFF[_n] = (_o, _w)
    _o += _w
NV = _o


def _fm(v):
    v = np.asarray(v, np.float32)
    return np.ascontiguousarray(v.reshape(-1, 128).T)


_RETIRED = {}


class Res:
    __slots__ = ("w", "r", "name")

    def __init__(self, name=""):
        self.w = None
        self.r = dict(_RETIRED)
        self.name = name


class Eng:
    def __init__(self, name, h, sem):
        self.name = name
        self.h = h
        self.sem = sem
        self.cnt = 0
        self.waited = {}
        self.pr = []
        self.pw = []


class KB:
    def __init__(self, nc, es):
        self.nc = nc
        self.es = es
        self.engs = {}
        for n in ["tensor", "vector", "scalar", "gpsimd", "sync"]:
            self.engs[n] = Eng(n, getattr(nc, n), es.enter_context(nc.semaphore("sem_" + n)))
        self.dsem = {}
        self.dcnt = {}
        for q in ["gpsimd", "sync"]:
            self.dsem[q] = [es.enter_context(nc.semaphore("dsem_%s_%d" % (q, i))) for i in range(NSEM_DMA)]
            self.dcnt[q] = 0
        self.out_toks = []
        self.scoped_dma = {}
        _RETIRED.clear()

    def retire(self):
        e = self.engs["tensor"]
        assert not e.pr and not e.pw
        for en in self.engs.values():
            if en.cnt > 0:
                _RETIRED[id(en.sem)] = (en.sem, en.cnt, None)
        for k, t in self.scoped_dma.items():
            _RETIRED[k] = t

    def _need(self, eng, tok):
        if tok is None:
            return
        sem, val, src = tok
        if src is eng:
            if eng.name == "tensor":
                return
            if eng.cnt - val >= 1:
                return
        k = id(sem)
        if eng.waited.get(k, 0) >= val:
            return
        eng.h.wait_ge(sem, val)
        eng.waited[k] = val

    def _deps(self, eng, reads, writes):
        for r in reads:
            self._need(eng, r.w)
        for w in writes:
            self._need(eng, w.w)
            for t in w.r.values():
                self._need(eng, t)

    def _commit(self, tok, reads, writes):
        k = id(tok[0])
        for r in reads:
            r.r[k] = tok
        for w in writes:
            w.w = tok
            w.r = {}

    def op(self, engname, fn, reads=(), writes=()):
        eng = self.engs[engname]
        self._deps(eng, reads, writes)
        inst = fn(eng.h)
        eng.cnt += 1
        inst.then_inc(eng.sem, 1)
        tok = (eng.sem, eng.cnt, eng)
        self._commit(tok, reads, writes)
        return tok

    def V(self, fn, reads=(), writes=()):
        return self.op("vector", fn, reads, writes)

    def A(self, fn, reads=(), writes=()):
        return self.op("scalar", fn, reads, writes)

    def mm(self, out, lhsT, rhs, start, stop, reads=(), writes=(), inc=None):
        if inc is None:
            inc = stop
        eng = self.engs["tensor"]
        self._deps(eng, reads, writes)
        inst = eng.h.matmul(out, lhsT=lhsT, rhs=rhs, start=start, stop=stop)
        eng.pr.extend(reads)
        eng.pw.extend(writes)
        if inc:
            eng.cnt += 1
            inst.then_inc(eng.sem, 1)
            tok = (eng.sem, eng.cnt, eng)
            self._commit(tok, eng.pr, eng.pw)
            eng.pr = []
            eng.pw = []

    def dma(self, q, out, in_, reads=(), writes=(), is_out=False, scoped=True, **kw):
        eng = self.engs[q]
        self._deps(eng, reads, writes)
        j = self.dcnt[q]
        self.dcnt[q] += 1
        sem = self.dsem[q][j % NSEM_DMA]
        val = 16 * (j // NSEM_DMA + 1)
        if j >= NSEM_DMA:
            self._need(eng, (sem, val - 16, None))
        eng.h.dma_start(out=out, in_=in_, **kw).then_inc(sem, 16)
        tok = (sem, val, None)
        self._commit(tok, reads, writes)
        if scoped:
            self.scoped_dma[id(sem)] = tok
        if is_out:
            self.out_toks.append(tok)
        return tok

    def all_tokens(self):
        toks = []
        for e in self.engs.values():
            if e.cnt > 0:
                toks.append((e.sem, e.cnt, e))
        for q in self.dsem:
            n = self.dcnt[q]
            for i in range(min(n, NSEM_DMA)):
                last_j = i + ((n - 1 - i) // NSEM_DMA) * NSEM_DMA
                toks.append((self.dsem[q][i], 16 * (last_j // NSEM_DMA + 1), None))
        return toks

    def barrier(self):
        assert not self.engs["tensor"].pr and not self.engs["tensor"].pw
        toks = self.all_tokens()
        for e in self.engs.values():
            for t in toks:
                if t[2] is e:
                    if e.name != "tensor" and e.cnt - t[1] < 2:
                        e.h.wait_ge(t[0], t[1])
                        e.waited[id(t[0])] = t[1]
                    continue
                self._need(e, t)

    def finish(self):
        e = self.engs["sync"]
        for t in self.all_tokens():
            if t[2] is e:
                continue
            self._need(e, t)


_UNIQ = [0]


def _uniq(name):
    _UNIQ[0] += 1
    return "t%d_%s" % (_UNIQ[0], name)


def interleave(gens):
    res = [None] * len(gens)
    alive = list(range(len(gens)))
    while alive:
        for i in list(alive):
            try:
                next(gens[i])
            except StopIteration as e:
                res[i] = e.value
                alive.remove(i)
    return res


class Pool:
    def __init__(self, kb, scope, name, shape, dt, n, psum=False):
        self.tiles = []
        for i in range(n):
            if psum:
                t = scope.enter_context(kb.nc.psum_tensor(_uniq("%s%d" % (name, i)), shape, dt))
            else:
                t = scope.enter_context(kb.nc.sbuf_tensor(_uniq("%s%d" % (name, i)), shape, dt))
            self.tiles.append((t, Res("%s%d" % (name, i))))
        self.i = 0

    def next(self):
        t = self.tiles[self.i % len(self.tiles)]
        self.i += 1
        return t


class Builder:
    def __init__(self, cfg, dbg=None):
        self.cfg = cfg
        self.nc = bass.Bass("TRN2", target_bir_lowering=False)
        self.dbg = dbg
        self.dbg_out = {}

    def dump(self, name, ap, shape, res):
        o = self.nc.dram_tensor("dbg_" + name, list(shape), F32, kind="ExternalOutput").ap()
        self.dbg_out[name] = o
        self.kb.dma("gpsimd", o, ap, reads=res, is_out=True)

    def dram_in(self, name, shape):
        return self.nc.dram_tensor(name, list(shape), F32, kind="ExternalInput").ap()

    def dram_out(self, name, shape):
        return self.nc.dram_tensor(name, list(shape), F32, kind="ExternalOutput").ap()

    def sb(self, scope, name, shape, dt=F32):
        return scope.enter_context(self.nc.sbuf_tensor(_uniq(name), list(shape), dt))

    def wload(self, src2d, nel):
        t, r = self.wpool.next()
        dst = t[:, 0:nel]
        if nel > 1024 and nel % 1024 == 0:
            self.kb.dma("gpsimd", dst.rearrange("p (a b) -> p a b", b=1024),
                        src2d.rearrange("p (a b) -> p a b", b=1024), writes=[r], scoped=False)
        else:
            self.kb.dma("gpsimd", dst, src2d, writes=[r], scoped=False)
        return t, r

    def build(self):
        cfg = self.cfg
        nc = self.nc
        L = cfg.depth
        TP, TS, PAST = cfg.tp, cfg.ts, cfg.past
        d = {}
        d["xp"] = self.dram_in("xp", [D, TP])
        d["xs"] = self.dram_in("xs", [D, TS])
        d["cvec"] = self.dram_in("cvec", [128, NCH, 2])
        d["vecs"] = self.dram_in("vecs", [L, 128, NV])
        d["h0s"] = self.dram_in("h0s", [L, 128, 2, NCH])
        d["sink"] = self.dram_in("sink", [L, 128, 8])
        d["w_ada"] = self.dram_in("w_ada", [L, 12, 128, 4096])
        d["w_in"] = self.dram_in("w_in", [L, 7, 128, 4096])
        d["w_g"] = self.dram_in("w_g", [L, 8, 128, 3072])
        d["w_br"] = self.dram_in("w_br", [L, 8, 128, 3072])
        d["w_out"] = self.dram_in("w_out", [L, 2, 128, 4096])
        d["w_up"] = self.dram_in("w_up", [L, 11, 128, 4096])
        d["w_dn"] = self.dram_in("w_dn", [L, 8, 128, 2816])
        d["w_lru"] = self.dram_in("w_lru", [L, 8, 128, 512])
        d["w_pool"] = self.dram_in("w_pool", [L, 4, 128, 512])
        d["kcT"] = self.dram_in("kcT", [L, 128, 2 * PAST])
        d["vc"] = self.dram_in("vc", [L, 128, (PAST // 128) * 256])
        d["perm"] = self.dram_in("perm", [128, 128])
        d["masks"] = self.dram_in("masks", [128, 2 * 512])
        d["rope"] = self.dram_in("rope", [2, 128, TS])
        d["pedge"] = self.dram_in("pedge", [128, 4 * 16])
        d["yp"] = self.dram_out("yp", [D, TP])
        d["ys"] = self.dram_out("ys", [D, TS])
        d["ko"] = self.dram_out("ko", [L, TP, 256])
        d["vo"] = self.dram_out("vo", [L, TP, 256])
        d["ho"] = self.dram_out("ho", [128, L * 2 * NCH * cfg.np_seq])
        self.d = d

        with contextlib.ExitStack() as es:
            self.kb = kb = KB(nc, es)
            self.es = es
            self.psum = Pool(kb, es, "ps", [128, 512], F32, 4, psum=True)
            self.pacc = Pool(kb, es, "pa", [128, 512], F32, 4, psum=True)
            self.wpool = Pool(kb, es, "wb", [128, WBUF_EL], BF16, NWBUF)
            self.n_sq = Pool(kb, es, "nsq", [128, NCH, 512], BF16, 1)
            self.n_rs = Pool(kb, es, "nrs", [128, 512], F32, 2)
            self.n_tt = Pool(kb, es, "ntt", [128, 512], F32, 2)
            self.setup_consts(es)
            self.par = self.sb(es, "par", [128, L, 2, 6, NCH], F32)
            self.compute_mod(0)
            with contextlib.ExitStack() as gs:
                self.run_group(gs, "P")
                kb.retire()
            with contextlib.ExitStack() as gs:
                self.run_group(gs, "S")
                kb.retire()
            kb.finish()
        return nc

    def setup_consts(self, es):
        cfg, kb, nc, d = self.cfg, self.kb, self.nc, self.d
        L = cfg.depth
        self.R_c = Res("consts")
        Rc = self.R_c
        self.ones_m = self.sb(es, "ones_m", [128, 128], BF16)
        self.ones_1 = self.sb(es, "ones_1", [128, 128], BF16)
        kb.V(lambda e: e.memset(self.ones_m[:], 1.0 / D), writes=[Rc])
        kb.V(lambda e: e.memset(self.ones_1[:], 1.0), writes=[Rc])
        self.perm = self.sb(es, "perm", [128, 128], F32)
        kb.dma("sync", self.perm[:], d["perm"][:], writes=[Rc])
        self.masks = self.sb(es, "masks", [128, 2 * 512], BF16)
        kb.dma("gpsimd", self.masks[:], d["masks"][:], writes=[Rc])
        self.pedge = self.sb(es, "pedge", [128, 64], F32)
        kb.dma("sync", self.pedge[:], d["pedge"][:], writes=[Rc])
        self.vec = self.sb(es, "vec", [128, L, NV], F32)
        for l in range(L):
            kb.dma("sync", self.vec[:, l, :], d["vecs"][l], writes=[Rc])
        self.h0s = self.sb(es, "h0s", [128, L, 2, NCH], F32)
        for l in range(L):
            kb.dma("sync", self.h0s[:, l], d["h0s"][l], writes=[Rc])
        self.esink = self.sb(es, "esink", [128, L, 8], F32)
        for l in range(L):
            kb.dma("sync", self.esink[:, l, :], d["sink"][l], writes=[Rc])
        kb.A(lambda e: e.activation(out=self.esink[:], in_=self.esink[:], func=AF.Exp), reads=[Rc], writes=[Rc])
        self.der = self.sb(es, "der", [128, L, 8, NCH], F32)
        self.bgh = self.sb(es, "bgh", [128, L, 24], F32)
        tmp = self.sb(es, "lamtmp", [128, 2, NCH], F32)
        tmp2 = self.sb(es, "lamtmp2", [128, 2, NCH], F32)
        for l in range(L):
            for i, nm in enumerate(["ba0", "ba1", "bx0", "bx1"]):
                kb.V(lambda e, i=i, nm=nm: e.tensor_scalar(out=self.der[:, l, i, :], in0=self.vv(l, nm), scalar1=-1.0,
                                                           scalar2=None, op0=ALU.mult), reads=[Rc], writes=[Rc])
            kb.V(lambda e: e.tensor_scalar(out=self.bgh[:, l, :], in0=self.vv(l, "b_gate"), scalar1=0.5, scalar2=None,
                                           op0=ALU.mult), reads=[Rc], writes=[Rc])
            o0 = VEC_OFF["lam0"][0]
            lam = self.vec[:, l, o0:o0 + 16].rearrange("p (a b) -> p a b", a=2)
            kb.A(lambda e: e.activation(out=tmp[:], in_=lam, func=AF.Exp, scale=-1.0), reads=[Rc], writes=[Rc])
            kb.V(lambda e: e.tensor_scalar(out=tmp2[:], in0=tmp[:], scalar1=-0.25, scalar2=1.0 / 3.0, op0=ALU.mult,
                                           op1=ALU.add), reads=[Rc], writes=[Rc])
            kb.V(lambda e: e.tensor_tensor(out=tmp2[:], in0=tmp2[:], in1=tmp[:], op=ALU.mult), reads=[Rc], writes=[Rc])
            kb.V(lambda e: e.tensor_scalar(out=tmp2[:], in0=tmp2[:], scalar1=-1.0, scalar2=0.5, op0=ALU.mult,
                                           op1=ALU.add), reads=[Rc], writes=[Rc])
            kb.V(lambda e: e.tensor_tensor(out=tmp2[:], in0=tmp2[:], in1=tmp[:], op=ALU.mult), reads=[Rc], writes=[Rc])
            kb.V(lambda e: e.tensor_scalar(out=tmp2[:], in0=tmp2[:], scalar1=-1.0, scalar2=1.0, op0=ALU.mult,
                                           op1=ALU.add), reads=[Rc], writes=[Rc])
            kb.V(lambda e: e.tensor_tensor(out=tmp2[:], in0=tmp2[:], in1=tmp[:], op=ALU.mult), reads=[Rc], writes=[Rc])
            kb.V(lambda e: e.tensor_scalar(out=self.der[:, l, 4:6, :], in0=tmp2[:], scalar1=-8.0, scalar2=None,
                                           op0=ALU.mult), reads=[Rc], writes=[Rc])
            kb.V(lambda e: e.tensor_scalar(out=self.der[:, l, 6:8, :], in0=tmp2[:], scalar1=-16.0, scalar2=None,
                                           op0=ALU.mult), reads=[Rc], writes=[Rc])

    def vv(self, l, name):
        o, w = VEC_OFF[name]
        return self.vec[:, l, o:o + w]

    def vcol(self, l, name, c):
        o, w = VEC_OFF[name]
        return self.vec[:, l, o + c:o + c + 1]

    def compute_mod(self, l):
        cfg, kb, nc, d = self.cfg, self.kb, self.nc, self.d
        L = cfg.depth
        Rc = self.R_c
        with contextlib.ExitStack() as sc:
            cv = self.sb(sc, "cv", [128, NCH, 2], F32)
            cvb = self.sb(sc, "cvb", [128, NCH, 2], BF16)
            modT = self.sb(sc, "modT", [128, 48, 2], F32)
            Rm = Res("mod")
            kb.dma("sync", cv[:], d["cvec"][:], writes=[Rm])
            kb.A(lambda e: e.activation(out=cvb[:], in_=cv[:], func=AF.Silu), reads=[Rm], writes=[Rm])
            if True:
                pt, pr = self.psum.next()
                for st in range(12):
                    wt, wr = self.wload(d["w_ada"][l, st], 4096)
                    w3 = wt[:, 0:4096].rearrange("p (k n) -> p k n", k=NCH)
                    for cc in range(4):
                        j = st * 4 + cc
                        for kc in range(NCH):
                            kb.mm(pt[:, 2 * j:2 * j + 2], w3[:, kc, cc * 128:(cc + 1) * 128], cvb[:, kc, :],
                                  start=(kc == 0), stop=(kc == NCH - 1), reads=[wr, Rm], writes=[pr],
                                  inc=(kc == NCH - 1 and cc == 3))
                kb.A(lambda e: e.activation(out=modT[:].rearrange("p a b -> p (a b)"), in_=pt[:, 0:96], func=AF.Copy),
                     reads=[pr], writes=[Rm])
                for g in range(2):
                    kb.V(lambda e, g=g: e.tensor_tensor(out=modT[:, :, g], in0=modT[:, :, g], in1=self.vv(l, "b_ada"),
                                                        op=ALU.add), reads=[Rm, Rc], writes=[Rm])
                    P = self.par
                    kb.V(lambda e, g=g: e.scalar_tensor_tensor(out=P[:, l, g, 0, :], in0=modT[:, 8:16, g], scalar=1.0,
                                                               in1=self.vv(l, "norm1"), op0=ALU.add, op1=ALU.mult),
                         reads=[Rm, Rc], writes=[Rc])
                    kb.V(lambda e, g=g: e.tensor_copy(out=P[:, l, g, 1, :], in_=modT[:, 0:8, g]), reads=[Rm], writes=[Rc])
                    kb.V(lambda e, g=g: e.tensor_scalar(out=P[:, l, g, 2, :], in0=modT[:, 16:24, g], scalar1=0.5,
                                                        scalar2=None, op0=ALU.mult), reads=[Rm], writes=[Rc])
                    kb.V(lambda e, g=g: e.scalar_tensor_tensor(out=P[:, l, g, 3, :], in0=modT[:, 32:40, g], scalar=1.0,
                                                               in1=self.vv(l, "norm2"), op0=ALU.add, op1=ALU.mult),
                         reads=[Rm, Rc], writes=[Rc])
                    kb.V(lambda e, g=g: e.tensor_copy(out=P[:, l, g, 4, :], in_=modT[:, 24:32, g]), reads=[Rm], writes=[Rc])
                    kb.V(lambda e, g=g: e.tensor_copy(out=P[:, l, g, 5, :], in_=modT[:, 40:48, g]), reads=[Rm], writes=[Rc])
            kb.retire()

    def norm_cols(self, x, xR, lo, hi, A, B, out_fn, outR, pools, done_fn=None):
        self._norm_multi(x, [xR] if isinstance(xR, Res) else list(xR), lo, hi, A, B, out_fn, outR, pools, done_fn)

    def lru_piece(self, l, c, dr, lw, lwr, xcv, xcb, xR, lo, w, pools):
        kb = self.kb
        Rc = self.R_c
        pz, pzr = self.psum.next()
        kb.mm(pz[:, 0:w], lw[:, dr, 0, :], xcb[:, lo:lo + w], True, True, reads=[lwr, xR], writes=[pzr])
        pi, pir = self.psum.next()
        kb.mm(pi[:, 0:w], lw[:, dr, 1, :], xcb[:, lo:lo + w], True, True, reads=[lwr, xR], writes=[pir])
        yield
        er, err = pools["f512"].next()
        ei, eir = pools["f512"].next()
        dv = self.der
        kb.A(lambda e: e.activation(out=er[:, 0:w], in_=pz[:, 0:w], func=AF.Exp, scale=-1.0,
                                    bias=dv[:, l, 0 + dr, c:c + 1]), reads=[pzr, Rc], writes=[err])
        yield
        kb.A(lambda e: e.activation(out=ei[:, 0:w], in_=pi[:, 0:w], func=AF.Exp, scale=-1.0,
                                    bias=dv[:, l, 2 + dr, c:c + 1]), reads=[pir, Rc], writes=[eir])
        yield
        kb.A(lambda e: e.activation(out=er[:, 0:w], in_=er[:, 0:w], func=AF.Ln, bias=1.0),
             reads=[err, Rc], writes=[err])
        yield
        kb.A(lambda e: e.activation(out=er[:, 0:w], in_=er[:, 0:w], func=AF.Exp, scale=-1.0), reads=[err], writes=[err])
        yield
        kb.A(lambda e: e.activation(out=ei[:, 0:w], in_=ei[:, 0:w], func=AF.Ln, bias=1.0),
             reads=[eir, Rc], writes=[eir])
        yield
        at, ar = pools["f512"].next()
        st, sr = pools["f512"].next()
        kb.A(lambda e: e.activation(out=at[:, 0:w], in_=er[:, 0:w], func=AF.Exp, scale=dv[:, l, 4 + dr, c:c + 1]),
             reads=[err, Rc], writes=[ar])
        yield
        kb.V(lambda e: e.tensor_tensor(out=st[:, 0:w], in0=at[:, 0:w], in1=at[:, 0:w], op=ALU.mult), reads=[ar], writes=[sr])
        yield
        kb.A(lambda e: e.activation(out=st[:, 0:w], in_=st[:, 0:w], func=AF.Ln, scale=-1.0, bias=1.0),
             reads=[sr, Rc], writes=[sr])
        yield
        kb.V(lambda e: e.scalar_tensor_tensor(out=st[:, 0:w], in0=st[:, 0:w], scalar=0.5, in1=ei[:, 0:w],
                                              op0=ALU.mult, op1=ALU.subtract), reads=[sr, eir], writes=[sr])
        yield
        kb.A(lambda e: e.activation(out=st[:, 0:w], in_=st[:, 0:w], func=AF.Exp), reads=[sr], writes=[sr])
        yield
        kb.V(lambda e: e.tensor_tensor(out=ei[:, 0:w], in0=st[:, 0:w], in1=xcv[:, lo:lo + w], op=ALU.mult),
             reads=[sr, xR, eir], writes=[eir])
        yield
        return at, ar, ei, eir

    def conv_lru(self, l, c, xa_pad, xaR, nseg, Lb, xcv, xcb, xR):
        kb = self.kb
        Rc = self.R_c
        xo = xcv[:, 0:nseg * Lb].rearrange("p (s t) -> p s t", s=nseg)
        kb.V(lambda e: e.tensor_scalar(out=xo, in0=xa_pad[:, :, 0:Lb], scalar1=self.vcol(l, "conv0", c),
                                       scalar2=self.vcol(l, "conv_b", c), op0=ALU.mult, op1=ALU.add),
             reads=[xaR, Rc], writes=[xR])
        for k in range(1, 4):
            kb.V(lambda e, k=k: e.scalar_tensor_tensor(out=xo, in0=xa_pad[:, :, k:k + Lb],
                                                       scalar=self.vcol(l, "conv%d" % k, c), in1=xo,
                                                       op0=ALU.mult, op1=ALU.add), reads=[xaR, xR, Rc], writes=[xR])
        kb.V(lambda e: e.tensor_copy(out=xcb[:, 0:nseg * Lb], in_=xcv[:, 0:nseg * Lb]), reads=[xR], writes=[xR])

    def run_group(self, gs, G):
        cfg, kb, nc, d = self.cfg, self.kb, self.nc, self.d
        L = cfg.depth
        isS = (G == "S")
        T = cfg.ts if isS else cfg.tp
        g = 1 if isS else 0
        nblk = T // TB
        Rc = self.R_c
        if not hasattr(self, "eps_t"):
            self.eps_t = self.sb(self.es, "eps_t", [128, 1], F32)
            self.one_t = self.sb(self.es, "one_t", [128, 1], F32)
            kb.V(lambda e: e.memset(self.eps_t[:], EPS), writes=[Rc])
            kb.V(lambda e: e.memset(self.one_t[:], 1.0), writes=[Rc])
        x = self.sb(gs, "x" + G, [128, NCH, T], F32)
        xR = [Res("x%s_%d" % (G, b)) for b in range(nblk)]
        src = d["xs"] if isS else d["xp"]
        for b in range(nblk):
            kb.dma("sync", x[:, :, b * TB:(b + 1) * TB],
                   src.rearrange("(c p) t -> p c t", p=128)[:, :, b * TB:(b + 1) * TB], writes=[xR[b]])
        if not isS:
            self.ho = self.sb(gs, "ho", [128, L * 2 * NCH * cfg.np_seq], F32)
            self.hoR = Res("ho")
        else:
            self.car = self.sb(gs, "car", [128, NCH, 2, nblk], F32)
            self.carR = Res("car")
            self.hprev = self.sb(gs, "hprev", [128, NCH, 128], BF16)
            self.hprevR = Res("hprev")
        for l in range(L):
            if isS:
                self.lru_pass1(l, x, xR, T)
            for b in range(nblk):
                self.mixer_block(l, G, x, xR, T, b)
                if (not isS) and l == 0 and b == 0:
                    for l2 in range(1, L):
                        self.compute_mod(l2)
            self.ffn_phase(l, G, x, xR, T)
        with contextlib.ExitStack() as sc:
            pools = {}
            ypool = Pool(kb, sc, "yo", [128, NCH, 512], F32, 2)
            dst = d["ys"] if isS else d["yp"]
            fo = VEC_OFF["final_norm"][0]
            ydst = dst.rearrange("(c p) t -> p c t", p=128)
            self._final_norm(x, xR, T, self.vec[:, 0, fo:fo + 8], ypool, ydst)
            if not isS:
                kb.dma("sync", d["ho"][:], self.ho[:], reads=[self.hoR], is_out=True)
            kb.retire()

    def lru_pass1(self, l, x, xR, T):
        cfg, kb, nc, d = self.cfg, self.kb, self.nc, self.d
        Rc = self.R_c
        nblk = T // TB
        par = self.par
        T0 = TB - 2
        TW = T - TB
        with contextlib.ExitStack() as sc:
            pools = {"f512": Pool(kb, sc, "p1f", [128, 512], F32, 8)}
            h = self.sb(sc, "p1h", [128, NCH, T - T0], BF16)
            hR = Res("p1h")
            self.norm_cols(x, list(xR), T0, T, par[:, l, 1, 0, :], par[:, l, 1, 1, :],
                           lambda c, off, w: h[:, c, off:off + w], hR, pools)
            xa_pool = Pool(kb, sc, "p1xa", [128, 1, TW + 3], F32, 2)
            xcv_pool = Pool(kb, sc, "p1xcv", [128, TW], F32, 2)
            xcb_pool = Pool(kb, sc, "p1xcb", [128, TW], BF16, 2)
            lwp = Pool(kb, sc, "p1lw", [128, 512], BF16, 2)
            scp = Pool(kb, sc, "p1sc", [128, 512], F32, 3)
            wts = {}

            def front(c):
                st, cc = c // 4, c % 4
                if st not in wts:
                    wt, wr = self.wload(d["w_in"][l, st], 4096)
                    wts[st] = (wt[:, 0:4096].rearrange("p (k n) -> p k n", k=NCH), wr)
                w3, wr = wts[st]
                lwt, lwr = lwp.next()
                kb.dma("gpsimd", lwt[:], d["w_lru"][l, c], writes=[lwr])
                lw = lwt[:].rearrange("p (d g n) -> p d g n", d=2, g=2)
                xa, xaR = xa_pool.next()
                kb.V(lambda e: e.memset(xa[:, 0, TW + 2:TW + 3], 0.0), writes=[xaR])
                lhs = w3[:, :, cc * 128:(cc + 1) * 128]
                pt, pr = self.pacc.next()
                for kc in range(NCH):
                    kb.mm(pt[:, 0:2], lhs[:, kc, :], h[:, kc, 0:2], start=(kc == 0), stop=(kc == NCH - 1),
                          reads=[wr, hR], writes=[pr])
                kb.V(lambda e, pt=pt: e.tensor_copy(out=xa[:, 0, 0:2], in_=pt[:, 0:2]), reads=[pr], writes=[xaR])
                for b in range(1, nblk):
                    pt, pr = self.pacc.next()
                    c0 = b * TB - T0
                    for kc in range(NCH):
                        kb.mm(pt[:], lhs[:, kc, :], h[:, kc, c0:c0 + TB], start=(kc == 0), stop=(kc == NCH - 1),
                              reads=[wr, hR], writes=[pr])
                    kb.V(lambda e, b=b, pt=pt: e.tensor_copy(out=xa[:, 0, 2 + (b - 1) * TB:2 + b * TB], in_=pt[:]),
                         reads=[pr], writes=[xaR])
                xcv, xcR = xcv_pool.next()
                xcb, _ = xcb_pool.next()
                self.conv_lru(l, c, xa, xaR, 1, TW, xcv, xcb, xcR)
                return lw, lwr, xcv, xcb, xcR

            def body(c, fr):
                lw, lwr, xcv, xcb, xcR = fr
                order = list(range(nblk - 1, 0, -1))
                prev = None
                pcs = {}
                for oi, b in enumerate(order):
                    if oi % 2 == 0:
                        grp = order[oi:oi + 2]
                        rs = interleave([self.lru_piece(l, c, 1, lw, lwr, xcv, xcb, xcR, (bb - 1) * TB, TB, pools)
                                         for bb in grp])
                        for bb, r_ in zip(grp, rs):
                            pcs[bb] = r_
                    at, ar, bt, br = pcs.pop(b)
                    ht, hr = scp.next()
                    if prev is None:
                        init = self.h0s[:, l, 1, c:c + 1]
                        ir = Rc
                    else:
                        init = prev[0][:, 0:1]
                        ir = prev[1]
                    kb.V(lambda e, ht=ht, at=at, bt=bt, init=init: e.tensor_tensor_scan(
                        out=ht[:, ::-1], data0=at[:, ::-1], data1=bt[:, ::-1], initial=init,
                        op0=ALU.mult, op1=ALU.add), reads=[ar, br, ir], writes=[hr])
                    kb.A(lambda e, ht=ht, b=b: e.activation(out=self.car[:, c, 1, b:b + 1], in_=ht[:, 0:1], func=AF.Copy),
                         reads=[hr], writes=[self.carR])
                    prev = (ht, hr)

            fr = front(0)
            for c in range(NCH):
                fr_next = front(c + 1) if c + 1 < NCH else None
                body(c, fr)
                fr = fr_next
            kb.retire()

    def mixer_block(self, l, G, x, xR, T, b):
        cfg, kb, nc, d = self.cfg, self.kb, self.nc, self.d
        Rc = self.R_c
        isS = (G == "S")
        g = 1 if isS else 0
        par = self.par
        s, e_ = b * TB, (b + 1) * TB
        nblk = T // TB
        if isS:
            lo, hi = max(0, s - 128), min(T, e_ + 128)
            nseg, Lb = 1, TB
        else:
            lo, hi = s, e_
            nseg, Lb = TB // cfg.lp, cfg.lp
        W = hi - lo
        off = s - lo
        xdeps = [xR[bb] for bb in range(nblk) if bb * TB < hi and (bb + 1) * TB > lo]
        with contextlib.ExitStack() as sc:
            hb = self.sb(sc, "hb", [128, NCH, W], BF16)
            hR = Res("hb")
            ya = self.sb(sc, "ya", [128, NCH, TB], BF16)
            yaR = Res("ya")
            yb = self.sb(sc, "yb", [128, NCH, TB], BF16)
            ybR = Res("yb")
            yc = self.sb(sc, "yc", [128, NCH, TB], BF16)
            ycR = Res("yc")
            kT = self.sb(sc, "kT", [128, 2, W], BF16)
            kTR = Res("kT")
            vt = self.sb(sc, "vt", [128, W // 128, 256], BF16)
            vtR = Res("vt")
            f512 = Pool(kb, sc, "mf", [128, 512], F32, 8)
            with contextlib.ExitStack() as s1:
                pools = {"f512": f512}
                self._norm_multi(x, xdeps, s, hi, par[:, l, g, 0, :], par[:, l, g, 1, :],
                                 lambda c, o, w: hb[:, c, off + o:off + o + w], hR, pools)
                if off > 0:
                    kb.A(lambda e: e.activation(out=hb[:, :, 0:off], in_=self.hprev[:], func=AF.Copy),
                         reads=[self.hprevR], writes=[hR])
                kb.retire()
            with contextlib.ExitStack() as s2:
                pools = {"f512": f512}
                xa_pool = Pool(kb, s2, "mxa", [128, nseg, Lb + 3], F32, 2)
                xcv_pool = Pool(kb, s2, "mxcv", [128, TB], F32, 2)
                xcb_pool = Pool(kb, s2, "mxcb", [128, TB], BF16, 2)
                lwp = Pool(kb, s2, "mlw", [128, 512], BF16, 2)
                hfp = Pool(kb, s2, "mhf", [128, TB], F32, 2)
                hbp = Pool(kb, s2, "mhb", [128, TB], F32, 2)
                wts = {}

                def front(c):
                    st, cc = c // 4, c % 4
                    if st not in wts:
                        wt, wr = self.wload(d["w_in"][l, st], 4096)
                        wts[st] = (wt[:, 0:4096].rearrange("p (k n) -> p k n", k=NCH), wr)
                    w3, wr = wts[st]
                    lwt, lwr = lwp.next()
                    kb.dma("gpsimd", lwt[:], d["w_lru"][l, c], writes=[lwr])
                    lw = lwt[:].rearrange("p (d g n) -> p d g n", d=2, g=2)
                    xa, xaR = xa_pool.next()
                    lhs = w3[:, :, cc * 128:(cc + 1) * 128]
                    if isS:
                        c0 = max(0, s - 2)
                        c1 = min(T, e_ + 1)
                        if s == 0:
                            kb.V(lambda e: e.memset(xa[:, 0, 0:2], 0.0), writes=[xaR])
                        if e_ == T:
                            kb.V(lambda e: e.memset(xa[:, 0, Lb + 2:Lb + 3], 0.0), writes=[xaR])
                        pos = c0
                        while pos < c1:
                            w = min(512, c1 - pos)
                            pt, pr = self.pacc.next()
                            for kc in range(NCH):
                                kb.mm(pt[:, 0:w], lhs[:, kc, :], hb[:, kc, pos - lo:pos - lo + w],
                                      start=(kc == 0), stop=(kc == NCH - 1), reads=[wr, hR], writes=[pr])
                            dc = pos - (s - 2)
                            kb.V(lambda e, pt=pt, dc=dc, w=w: e.tensor_copy(out=xa[:, 0, dc:dc + w], in_=pt[:, 0:w]),
                                 reads=[pr], writes=[xaR])
                            pos += w
                    else:
                        kb.V(lambda e: e.memset(xa[:, :, 0:2], 0.0), writes=[xaR])
                        kb.V(lambda e: e.memset(xa[:, :, Lb + 2:Lb + 3], 0.0), writes=[xaR])
                        pt, pr = self.pacc.next()
                        for kc in range(NCH):
                            kb.mm(pt[:], lhs[:, kc, :], hb[:, kc, :], start=(kc == 0), stop=(kc == NCH - 1),
                                  reads=[wr, hR], writes=[pr])
                        kb.V(lambda e, pt=pt: e.tensor_copy(out=xa[:, :, 2:2 + Lb],
                                                            in_=pt[:].rearrange("p (s t) -> p s t", s=nseg)),
                             reads=[pr], writes=[xaR])
                    xcv, xcvR = xcv_pool.next()
                    xcb, _ = xcb_pool.next()
                    self.conv_lru(l, c, xa, xaR, nseg, Lb, xcv, xcb, xcvR)
                    return lw, lwr, xcv, xcb, xcvR

                def body(c, fr):
                    lw, lwr, xcv, xcb, xcvR = fr
                    hf, hfR = hfp.next()
                    hbk, hbR = hbp.next()
                    pres = interleave([self.lru_piece(l, c, dr, lw, lwr, xcv, xcb, xcvR, 0, TB, pools)
                                       for dr in range(2)])
                    return hf, hfR, hbk, hbR, pres

                def back(c, bd):
                    hf, hfR, hbk, hbR, pres = bd
                    for dr in range(2):
                        at, ar, bt, br = pres[dr]
                        for sg in range(nseg):
                            q0, q1 = sg * Lb, (sg + 1) * Lb
                            if isS:
                                if dr == 0:
                                    init = self.h0s[:, l, 0, c:c + 1] if b == 0 else self.car[:, c, 0, b - 1:b]
                                else:
                                    init = self.h0s[:, l, 1, c:c + 1] if b == nblk - 1 else self.car[:, c, 1, b + 1:b + 2]
                                ideps = [Rc, self.carR]
                            else:
                                init = 0.0
                                ideps = []
                            if dr == 0:
                                kb.V(lambda e, init=init, q0=q0, q1=q1: e.tensor_tensor_scan(
                                    out=hf[:, q0:q1], data0=at[:, q0:q1], data1=bt[:, q0:q1], initial=init,
                                    op0=ALU.mult, op1=ALU.add), reads=[ar, br] + ideps, writes=[hfR])
                            else:
                                kb.V(lambda e, init=init, q0=q0, q1=q1: e.tensor_tensor_scan(
                                    out=hbk[:, q0:q1][:, ::-1], data0=at[:, q0:q1][:, ::-1],
                                    data1=bt[:, q0:q1][:, ::-1], initial=init,
                                    op0=ALU.mult, op1=ALU.add), reads=[ar, br] + ideps, writes=[hbR])
                    kb.V(lambda e: e.tensor_tensor(out=ya[:, c, :], in0=hf[:], in1=hbk[:], op=ALU.add),
                         reads=[hfR, hbR], writes=[yaR])
                    if isS and b < nblk - 1:
                        kb.V(lambda e: e.tensor_copy(out=self.car[:, c, 0, b:b + 1], in_=hf[:, TB - 1:TB]),
                             reads=[hfR], writes=[self.carR])
                    if not isS:
                        for sg in range(nseg):
                            seq = b * nseg + sg
                            i0 = ((l * 2 + 0) * NCH + c) * cfg.np_seq + seq
                            i1 = ((l * 2 + 1) * NCH + c) * cfg.np_seq + seq
                            kb.A(lambda e, i0=i0, sg=sg: e.activation(out=self.ho[:, i0:i0 + 1],
                                                                      in_=hf[:, (sg + 1) * Lb - 1:(sg + 1) * Lb],
                                                                      func=AF.Copy), reads=[hfR], writes=[self.hoR])
                            kb.A(lambda e, i1=i1, sg=sg: e.activation(out=self.ho[:, i1:i1 + 1],
                                                                      in_=hbk[:, sg * Lb:sg * Lb + 1],
                                                                      func=AF.Copy), reads=[hbR], writes=[self.hoR])

                fr = front(0)
                for c in range(NCH):
                    fr_next = front(c + 1) if c + 1 < NCH else None
                    bd = body(c, fr)
                    back(c, bd)
                    fr = fr_next
                kb.retire()
            with contextlib.ExitStack() as s3:
                wt, wr = self.wload(d["w_in"][l, 4], 4096)
                w3 = wt[:, 0:4096].rearrange("p (k n) -> p k n", k=NCH)
                if isS:
                    rope = self.sb(s3, "ropek", [128, 2, W], F32)
                    rR = Res("ropek")
                    kb.dma("sync", rope[:], d["rope"].rearrange("a p t -> p a t")[:, :, lo:hi], writes=[rR])
                    kf_pool = Pool(kb, s3, "kf", [128, 512], F32, 2)
                for kv in range(2):
                    pos = 0
                    while pos < W:
                        w = min(512, W - pos)
                        pt, pr = self.psum.next()
                        for kc in range(NCH):
                            kb.mm(pt[:, 0:w], w3[:, kc, kv * 128:(kv + 1) * 128], hb[:, kc, pos:pos + w],
                                  start=(kc == 0), stop=(kc == NCH - 1), reads=[wr, hR], writes=[pr])
                        if isS:
                            kf, kfr = kf_pool.next()
                            kb.A(lambda e, pt=pt, w=w, kf=kf: e.activation(out=kf[:, 0:w], in_=pt[:, 0:w], func=AF.Copy),
                                 reads=[pr], writes=[kfr])
                            self.rope_apply(kf, kfr, w, rope[:, 0, pos:pos + w], rope[:, 1, pos:pos + w], rR,
                                            kT[:, kv, pos:pos + w], kTR, f512)
                        else:
                            kb.A(lambda e, pt=pt, w=w, kv=kv, pos=pos: e.activation(out=kT[:, kv, pos:pos + w],
                                                                                    in_=pt[:, 0:w], func=AF.Copy),
                                 reads=[pr], writes=[kTR])
                        pos += w
                if not isS:
                    kvo_pool = Pool(kb, s3, "kvo", [128, 512], F32, 2)
                for ch in range(W // 128):
                    pt, pr = self.psum.next()
                    if isS:
                        for kc in range(NCH):
                            kb.mm(pt[:, 0:256], hb[:, kc, ch * 128:(ch + 1) * 128], w3[:, kc, 256:512],
                                  start=(kc == 0), stop=(kc == NCH - 1), reads=[wr, hR], writes=[pr])
                        kb.A(lambda e, pt=pt, ch=ch: e.activation(out=vt[:, ch, :], in_=pt[:, 0:256], func=AF.Copy),
                             reads=[pr], writes=[vtR])
                    else:
                        for kc in range(NCH):
                            kb.mm(pt[:], hb[:, kc, ch * 128:(ch + 1) * 128], w3[:, kc, :],
                                  start=(kc == 0), stop=(kc == NCH - 1), reads=[wr, hR], writes=[pr])
                        ko, kor = kvo_pool.next()
                        kb.A(lambda e, pt=pt, ko=ko: e.activation(out=ko[:], in_=pt[:], func=AF.Copy),
                             reads=[pr], writes=[kor])
                        kb.V(lambda e, ko=ko, ch=ch: e.tensor_copy(out=vt[:, ch, :], in_=ko[:, 256:512]),
                             reads=[kor], writes=[vtR])
                        t0 = s + ch * 128
                        kb.dma("sync", d["ko"][l, t0:t0 + 128, :], ko[:, 0:256], reads=[kor], is_out=True)
                        kb.dma("sync", d["vo"][l, t0:t0 + 128, :], ko[:, 256:512], reads=[kor], is_out=True)
                kb.retire()
            with contextlib.ExitStack() as s4:
                if isS:
                    kc_t = self.sb(s4, "kcT", [128, 2, cfg.past], BF16)
                    vc_t = self.sb(s4, "vcT", [128, cfg.past // 128, 256], BF16)
                    cR = Res("ctx")
                    kb.dma("gpsimd", kc_t[:].rearrange("p a t -> p (a t)"), d["kcT"][l], writes=[cR])
                    kb.dma("gpsimd", vc_t[:].rearrange("p a t -> p (a t)"), d["vc"][l], writes=[cR])
                    ropeq = self.sb(s4, "ropeq", [128, 2, TB], F32)
                    rqR = Res("ropeq")
                    kb.dma("sync", ropeq[:], d["rope"].rearrange("a p t -> p a t")[:, :, s:e_], writes=[rqR])
                    qf_pool = Pool(kb, s4, "qf", [128, 512], F32, 2)
                qg_pool = Pool(kb, s4, "qg", [128, 4, TB], BF16, 2)
                pT_pool = Pool(kb, s4, "pT", [128, 512], BF16, 3)
                for kvg in range(2):
                    wt, wr = self.wload(d["w_in"][l, 2 + kvg], 4096)
                    w3 = wt[:, 0:4096].rearrange("p (k n) -> p k n", k=NCH)
                    qg, qgR = qg_pool.next()
                    for hh in range(4):
                        pt, pr = self.psum.next()
                        for kc in range(NCH):
                            kb.mm(pt[:], w3[:, kc, hh * 128:(hh + 1) * 128], hb[:, kc, off:off + TB],
                                  start=(kc == 0), stop=(kc == NCH - 1), reads=[wr, hR], writes=[pr])
                        if isS:
                            qf, qfr = qf_pool.next()
                            kb.A(lambda e, pt=pt, qf=qf: e.activation(out=qf[:], in_=pt[:], func=AF.Copy),
                                 reads=[pr], writes=[qfr])
                            self.rope_apply(qf, qfr, TB, ropeq[:, 0, :], ropeq[:, 1, :], rqR, qg[:, hh, :], qgR, f512)
                        else:
                            kb.A(lambda e, pt=pt, hh=hh: e.activation(out=qg[:, hh, :], in_=pt[:], func=AF.Copy),
                                 reads=[pr], writes=[qgR])
                    for j in range(TB // 128):
                        jq = (s // 128) + j
                        chunks = []
                        if isS:
                            for dj, mk in ((-1, 0), (0, None), (1, 1)):
                                jj = jq + dj
                                if jj < 0 or jj >= T // 128:
                                    continue
                                col = jj * 128 - lo
                                chunks.append((kT[:, kvg, col:col + 128], vt[:, col // 128, kvg * 128:(kvg + 1) * 128],
                                               mk, [kTR, vtR]))
                            for cj in range(cfg.past // 128):
                                chunks.append((kc_t[:, kvg, cj * 128:(cj + 1) * 128],
                                               vc_t[:, cj, kvg * 128:(kvg + 1) * 128], None, [cR]))
                        else:
                            sq0 = (jq * 128 // cfg.lp) * cfg.lp
                            for jj in range(cfg.lp // 128):
                                col = sq0 + jj * 128 - lo
                                chunks.append((kT[:, kvg, col:col + 128], vt[:, col // 128, kvg * 128:(kvg + 1) * 128],
                                               None, [kTR, vtR]))
                        po, por = self.pacc.next()
                        pd, pdr = self.pacc.next()
                        rhs_q = qg[:, :, j * 128:(j + 1) * 128]
                        n = len(chunks)
                        qk = {}

                        def issue_qk(ci):
                            kl, vl, mk, deps = chunks[ci]
                            ps_, psr = self.psum.next()
                            kb.mm(ps_[:].rearrange("p (a b) -> p a b", a=4), kl, rhs_q, True, True,
                                  reads=deps + [qgR], writes=[psr])
                            qk[ci] = (ps_, psr)

                        issue_qk(0)
                        if n > 1:
                            issue_qk(1)
                        for ci, (kl, vl, mk, deps) in enumerate(chunks):
                            ps_, psr = qk.pop(ci)
                            pT, pTr = pT_pool.next()
                            kb.A(lambda e, ps_=ps_, pT=pT: e.activation(out=pT[:], in_=ps_[:], func=AF.Exp,
                                                                        scale=float(128 ** -0.5)),
                                 reads=[psr], writes=[pTr])
                            if mk is not None:
                                kb.V(lambda e, pT=pT, mk=mk: e.tensor_tensor(out=pT[:], in0=pT[:],
                                                                             in1=self.masks[:, mk * 512:(mk + 1) * 512],
                                                                             op=ALU.mult), reads=[pTr, Rc], writes=[pTr])
                            if ci + 2 < n:
                                issue_qk(ci + 2)
                            kb.mm(po[:], vl, pT[:], start=(ci == 0), stop=(ci == n - 1), reads=deps + [pTr],
                                  writes=[por], inc=False)
                            kb.mm(pd[:], self.ones_1[:], pT[:], start=(ci == 0), stop=(ci == n - 1), reads=[pTr, Rc],
                                  writes=[pdr], inc=True)
                        rd, rdr = f512.next()
                        for hh in range(4):
                            hd = kvg * 4 + hh
                            kb.A(lambda e, hh=hh, hd=hd: e.activation(out=rd[:, hh * 128:(hh + 1) * 128],
                                                                      in_=pd[:, hh * 128:(hh + 1) * 128], func=AF.Ln,
                                                                      bias=self.esink[:, l, hd:hd + 1]),
                                 reads=[pdr, Rc], writes=[rdr])
                        kb.A(lambda e: e.activation(out=rd[:], in_=rd[:], func=AF.Exp, scale=-1.0), reads=[rdr], writes=[rdr])
                        kb.V(lambda e, j=j: e.tensor_tensor(out=yb[:, kvg * 4:(kvg + 1) * 4, j * 128:(j + 1) * 128],
                                                            in0=po[:].rearrange("p (a b) -> p a b", a=4),
                                                            in1=rd[:].rearrange("p (a b) -> p a b", a=4), op=ALU.mult),
                             reads=[por, rdr], writes=[ybR])
                kb.retire()
            with contextlib.ExitStack() as s5:
                PW = Lb + 16
                xc_pool = Pool(kb, s5, "xc", [128, nseg, PW], F32, 2)
                sA_pool = Pool(kb, s5, "sA", [128, nseg, PW], F32, 2)
                sB_pool = Pool(kb, s5, "sB", [128, nseg, PW], F32, 2)
                pl_pool = Pool(kb, s5, "pl", [128, 2, TB], BF16, 2)
                pwp = Pool(kb, s5, "pw", [128, 512], BF16, 2)
                pws = {}

                def pfront(gi):
                        win = 2 << gi
                        st = 5 + gi // 2
                        if st not in pws:
                            wt_, wr_ = self.wload(d["w_in"][l, st], 4096)
                            pws[st] = (wt_[:, 0:4096].rearrange("p (k n) -> p k n", k=NCH), wr_)
                        w3, wr = pws[st]
                        pwt, pwr = pwp.next()
                        kb.dma("gpsimd", pwt[:], d["w_pool"][l, gi], writes=[pwr])
                        pw3 = pwt[:].rearrange("p (i n) -> p i n", i=2)
                        pl, plR = pl_pool.next()
                        for ic in range(2):
                            cc = (gi % 2) * 2 + ic
                            xc, xcR = xc_pool.next()
                            lhs = w3[:, :, cc * 128:(cc + 1) * 128]
                            if isS:
                                c0 = max(0, s - 8)
                                c1 = min(T, e_ + 8)
                                if s == 0:
                                    kb.V(lambda e, xc=xc: e.memset(xc[:, 0, 0:8], 0.0), writes=[xcR])
                                if e_ == T:
                                    kb.V(lambda e, xc=xc: e.memset(xc[:, 0, Lb + 8:Lb + 16], 0.0), writes=[xcR])
                                pos = c0
                                while pos < c1:
                                    w = min(512, c1 - pos)
                                    pt, pr = self.psum.next()
                                    for kc in range(NCH):
                                        kb.mm(pt[:, 0:w], lhs[:, kc, :], hb[:, kc, pos - lo:pos - lo + w],
                                              start=(kc == 0), stop=(kc == NCH - 1), reads=[wr, hR], writes=[pr])
                                    dc = pos - (s - 8)
                                    kb.A(lambda e, pt=pt, dc=dc, w=w, xc=xc: e.activation(out=xc[:, 0, dc:dc + w],
                                                                                          in_=pt[:, 0:w], func=AF.Copy),
                                         reads=[pr], writes=[xcR])
                                    pos += w
                            else:
                                kb.V(lambda e, xc=xc: e.memset(xc[:, :, 0:8], 0.0), writes=[xcR])
                                kb.V(lambda e, xc=xc: e.memset(xc[:, :, Lb + 8:Lb + 16], 0.0), writes=[xcR])
                                pt, pr = self.psum.next()
                                for kc in range(NCH):
                                    kb.mm(pt[:], lhs[:, kc, :], hb[:, kc, :], start=(kc == 0), stop=(kc == NCH - 1),
                                          reads=[wr, hR], writes=[pr])
                                kb.A(lambda e, pt=pt, xc=xc: e.activation(out=xc[:, :, 8:8 + Lb],
                                                                          in_=pt[:].rearrange("p (s t) -> p s t", s=nseg),
                                                                          func=AF.Copy), reads=[pr], writes=[xcR])
                            cur, curR = xc, xcR
                            lo_u, hi_u = 0, PW
                            sh = 0
                            step = 0
                            while (1 << step) < win:
                                nxt, nxtR = (sA_pool if step % 2 == 0 else sB_pool).next()
                                if step == 0:
                                    a0, a1 = lo_u + 1, hi_u
                                    kb.V(lambda e, cur=cur, nxt=nxt, a0=a0, a1=a1: e.tensor_tensor(
                                        out=nxt[:, :, a0:a1], in0=cur[:, :, a0 - 1:a1 - 1], in1=cur[:, :, a0:a1], op=ALU.add),
                                        reads=[curR], writes=[nxtR])
                                    lo_u, hi_u = a0, a1
                                else:
                                    dlt = 1 << (step - 1)
                                    a0, a1 = lo_u + dlt, hi_u - dlt
                                    kb.V(lambda e, cur=cur, nxt=nxt, a0=a0, a1=a1, dlt=dlt: e.tensor_tensor(
                                        out=nxt[:, :, a0:a1], in0=cur[:, :, a0 - dlt:a1 - dlt], in1=cur[:, :, a0 + dlt:a1 + dlt],
                                        op=ALU.add), reads=[curR], writes=[nxtR])
                                    lo_u, hi_u = a0, a1
                                cur, curR = nxt, nxtR
                                step += 1
                            m = cur
                            kb.V(lambda e, m=m: e.tensor_scalar(out=m[:, :, 8:8 + Lb], in0=m[:, :, 8:8 + Lb], scalar1=1.0 / win,
                                                                scalar2=None, op0=ALU.mult), reads=[curR], writes=[curR])
                            segs_l = range(nseg) if not isS else ([0] if s == 0 else [])
                            segs_r = range(nseg) if not isS else ([0] if e_ == T else [])
                            for sg in segs_l:
                                kb.V(lambda e, m=m, sg=sg: e.tensor_tensor(out=m[:, sg, 8:16], in0=m[:, sg, 8:16],
                                                                           in1=self.pedge[:, gi * 16:gi * 16 + 8], op=ALU.mult),
                                     reads=[curR, Rc], writes=[curR])
                            for sg in segs_r:
                                kb.V(lambda e, m=m, sg=sg: e.tensor_tensor(out=m[:, sg, Lb:Lb + 8], in0=m[:, sg, Lb:Lb + 8],
                                                                           in1=self.pedge[:, gi * 16 + 8:gi * 16 + 16],
                                                                           op=ALU.mult), reads=[curR, Rc], writes=[curR])
                            kb.V(lambda e, m=m, xc=xc, ic=ic: e.tensor_tensor(
                                out=pl[:, ic, :].rearrange("p (s t) -> p s t", s=nseg), in0=m[:, :, 8:8 + Lb],
                                in1=xc[:, :, 8:8 + Lb], op=ALU.subtract), reads=[curR, xcR], writes=[plR])
                        return pw3, pwr, pl, plR

                def pback(gi, fr_):
                        pw3, pwr, pl, plR = fr_
                        for jc in range(2):
                            pt, pr = self.pacc.next()
                            for ic in range(2):
                                kb.mm(pt[:], pw3[:, ic, jc * 128:(jc + 1) * 128], pl[:, ic, :], start=(ic == 0), stop=(ic == 1),
                                      reads=[pwr, plR], writes=[pr])
                            oc = gi * 2 + jc
                            kb.A(lambda e, pt=pt, oc=oc: e.activation(out=yc[:, oc, :], in_=pt[:], func=AF.Identity,
                                                                      scale=self.vcol(l, "pool_scale", oc)),
                                 reads=[pr, Rc], writes=[ycR])

                pfr = {0: pfront(0)}
                for gi in range(4):
                    if gi + 1 < 4:
                        pfr[gi + 1] = pfront(gi + 1)
                    pback(gi, pfr.pop(gi))
                kb.retire()
            with contextlib.ExitStack() as s6:
                mg = self.sb(s6, "mg", [128, NCH, TB], BF16)
                mgR = Res("mg")
                gp = Pool(kb, s6, "gt", [128, TB], F32, 3)
                ap_ = Pool(kb, s6, "acc", [128, TB], F32, 2)
                tp_ = Pool(kb, s6, "tmp", [128, TB], F32, 2)
                ys = [(ya, yaR), (yb, ybR), (yc, ycR)]
                for m in range(NCH):
                    wg, wgr = self.wload(d["w_g"][l, m], 3072)
                    wg4 = wg[:, 0:3072].rearrange("p (b k n) -> p b k n", b=3, k=NCH)
                    wbt, wbr = self.wload(d["w_br"][l, m], 3072)
                    wb4 = wbt[:, 0:3072].rearrange("p (b k n) -> p b k n", b=3, k=NCH)
                    acc, accR = ap_.next()
                    for br_ in range(3):
                        pg, pgr = self.psum.next()
                        for kc in range(NCH):
                            kb.mm(pg[:], wg4[:, br_, kc, :], hb[:, kc, off:off + TB], start=(kc == 0),
                                  stop=(kc == NCH - 1), reads=[wgr, hR], writes=[pgr])
                        gt, gtr = gp.next()
                        bo = br_ * 8 + m
                        kb.A(lambda e, pg=pg, gt=gt, bo=bo: e.activation(out=gt[:], in_=pg[:], func=AF.Tanh, scale=0.5,
                                                                         bias=self.bgh[:, l, bo:bo + 1]),
                             reads=[pgr, Rc], writes=[gtr])
                        pb, pbr = self.psum.next()
                        yt, ytR = ys[br_]
                        for kc in range(NCH):
                            kb.mm(pb[:], wb4[:, br_, kc, :], yt[:, kc, :], start=(kc == 0), stop=(kc == NCH - 1),
                                  reads=[wbr, ytR], writes=[pbr])
                        if br_ == 0:
                            kb.V(lambda e, gt=gt, pb=pb, acc=acc: e.scalar_tensor_tensor(out=acc[:], in0=gt[:], scalar=1.0,
                                                                                         in1=pb[:], op0=ALU.add,
                                                                                         op1=ALU.mult),
                                 reads=[gtr, pbr], writes=[accR])
                        else:
                            tm, tmr = tp_.next()
                            kb.V(lambda e, gt=gt, pb=pb, tm=tm: e.scalar_tensor_tensor(out=tm[:], in0=gt[:], scalar=1.0,
                                                                                       in1=pb[:], op0=ALU.add,
                                                                                       op1=ALU.mult),
                                 reads=[gtr, pbr], writes=[tmr])
                            if br_ == 1:
                                kb.V(lambda e, tm=tm, acc=acc: e.tensor_tensor(out=acc[:], in0=acc[:], in1=tm[:], op=ALU.add),
                                     reads=[accR, tmr], writes=[accR])
                            else:
                                kb.V(lambda e, tm=tm, acc=acc, m=m: e.tensor_tensor(out=mg[:, m, :], in0=acc[:], in1=tm[:],
                                                                                    op=ALU.add),
                                     reads=[accR, tmr], writes=[mgR])
                for st in range(2):
                    wt, wr = self.wload(d["w_out"][l, st], 4096)
                    w3 = wt[:, 0:4096].rearrange("p (k n) -> p k n", k=NCH)
                    for cc in range(4):
                        m = st * 4 + cc
                        pt, pr = self.psum.next()
                        for kc in range(NCH):
                            kb.mm(pt[:], w3[:, kc, cc * 128:(cc + 1) * 128], mg[:, kc, :], start=(kc == 0),
                                  stop=(kc == NCH - 1), reads=[wr, mgR], writes=[pr])
                        kb.V(lambda e, pt=pt, m=m: e.scalar_tensor_tensor(out=x[:, m, s:e_], in0=pt[:],
                                                                          scalar=self.par[:, l, g, 2, m:m + 1],
                                                                          in1=x[:, m, s:e_], op0=ALU.mult, op1=ALU.add),
                             reads=[pr, Rc] + xdeps, writes=[xR[b]])
                if isS and b < nblk - 1:
                    kb.A(lambda e: e.activation(out=self.hprev[:], in_=hb[:, :, off + TB - 128:off + TB], func=AF.Copy),
                         reads=[hR], writes=[self.hprevR])
                if self.dbg == (G, l, b):
                    self.dump("hb", hb[:], [128, NCH, W], [hR])
                    self.dump("ya", ya[:], [128, NCH, TB], [yaR])
                    self.dump("yb", yb[:], [128, NCH, TB], [ybR])
                    self.dump("yc", yc[:], [128, NCH, TB], [ycR])
                    self.dump("mg", mg[:], [128, NCH, TB], [mgR])
                    self.dump("kT", kT[:], [128, 2, W], [kTR])
                    self.dump("xmid", x[:, :, s:e_], [128, NCH, TB], [xR[b]])
                kb.retire()

    def _norm_multi(self, x, xdeps, lo, hi, A, B, out_fn, outR, pools, done_fn=None):
        kb = self.kb
        Rc = self.R_c
        subs = []
        pos = lo
        while pos < hi:
            w = min(512, hi - pos)
            subs.append((pos, w))
            pos += w

        def stage_a(pos, w):
            sq, sqr = self.n_sq.next()
            kb.A(lambda e: e.activation(out=sq[:, :, 0:w], in_=x[:, :, pos:pos + w], func=AF.Square),
                 reads=xdeps, writes=[sqr])
            pt, pr = self.psum.next()
            for c in range(NCH):
                kb.mm(pt[:, 0:w], self.ones_m[:], sq[:, c, 0:w], start=(c == 0), stop=(c == NCH - 1),
                      reads=[sqr, Rc], writes=[pr])
            rt, rr = self.n_rs.next()
            kb.A(lambda e: e.activation(out=rt[:, 0:w], in_=pt[:, 0:w], func=AF.Ln, bias=EPS),
                 reads=[pr, Rc], writes=[rr])
            kb.A(lambda e: e.activation(out=rt[:, 0:w], in_=rt[:, 0:w], func=AF.Exp, scale=-0.5), reads=[rr], writes=[rr])
            return rt, rr

        def stage_b(pos, w, rt, rr):
            for c in range(NCH):
                o = out_fn(c, pos - lo, w)
                if B is None:
                    kb.V(lambda e: e.scalar_tensor_tensor(out=o, in0=x[:, c, pos:pos + w], scalar=A[:, c:c + 1],
                                                          in1=rt[:, 0:w], op0=ALU.mult, op1=ALU.mult),
                         reads=xdeps + [rr, Rc], writes=[outR])
                    continue
                tt, tr = self.n_tt.next()
                kb.V(lambda e: e.scalar_tensor_tensor(out=tt[:, 0:w], in0=x[:, c, pos:pos + w], scalar=A[:, c:c + 1],
                                                      in1=rt[:, 0:w], op0=ALU.mult, op1=ALU.mult),
                     reads=xdeps + [rr, Rc], writes=[tr])
                if c % 2 == 0:
                    kb.A(lambda e: e.activation(out=o, in_=tt[:, 0:w], func=AF.Identity, bias=B[:, c:c + 1]),
                         reads=[tr, Rc], writes=[outR])
                else:
                    kb.V(lambda e: e.tensor_scalar(out=o, in0=tt[:, 0:w], scalar1=B[:, c:c + 1], scalar2=None,
                                                   op0=ALU.add), reads=[tr, Rc], writes=[outR])
            if done_fn is not None:
                done_fn(pos, w)

        cur = stage_a(*subs[0])
        for k, (pos, w) in enumerate(subs):
            nxt = stage_a(*subs[k + 1]) if k + 1 < len(subs) else None
            stage_b(pos, w, *cur)
            cur = nxt

    def _final_norm(self, x, xR, T, A, ypool, ydst):
        kb = self.kb
        for b in range(T // TB):
            yt, yr = ypool.next()
            self._norm_multi(x, [xR[b]], b * TB, (b + 1) * TB, A, None,
                             lambda c, off, w, yt=yt: yt[:, c, off:off + w], yr, None,
                             lambda pos, w, yt=yt, yr=yr: kb.dma("sync", ydst[:, :, pos:pos + w], yt[:, :, 0:w],
                                                                 reads=[yr], is_out=True))

    def rope_apply(self, qf, qfr, w, cos, sin, rR, out, outR, f512):
        kb = self.kb
        Rc = self.R_c
        pp, ppr = self.psum.next()
        kb.mm(pp[:, 0:w], self.perm[:], qf[:, 0:w], True, True, reads=[Rc, qfr], writes=[ppr])
        t1, t1r = f512.next()
        kb.V(lambda e: e.tensor_tensor(out=t1[:, 0:w], in0=pp[:, 0:w], in1=sin, op=ALU.mult), reads=[ppr, rR], writes=[t1r])
        kb.V(lambda e: e.tensor_tensor(out=qf[:, 0:w], in0=qf[:, 0:w], in1=cos, op=ALU.mult), reads=[qfr, rR], writes=[qfr])
        kb.V(lambda e: e.tensor_tensor(out=out, in0=qf[:, 0:w], in1=t1[:, 0:w], op=ALU.add), reads=[qfr, t1r],
             writes=[outR])

    def ffn_phase(self, l, G, x, xR, T):
        cfg, kb, nc, d = self.cfg, self.kb, self.nc, self.d
        Rc = self.R_c
        isS = (G == "S")
        g = 1 if isS else 0
        par = self.par
        nblk = T // TB
        with contextlib.ExitStack() as sc:
            h2 = self.sb(sc, "h2", [128, NCH, T], BF16)
            hR = Res("h2")
            f512 = Pool(kb, sc, "ff", [128, 512], F32, 6)
            with contextlib.ExitStack() as s1:
                pools = {"f512": f512}
                self._norm_multi(x, list(xR), 0, T, par[:, l, g, 3, :], par[:, l, g, 4, :],
                                 lambda c, o, w: h2[:, c, o:o + w], hR, pools)
                kb.retire()
            if isS:
                nseg, Lb = 1, TB
            else:
                nseg, Lb = TB // cfg.lp, cfg.lp
            act = self.sb(sc, "act", [128, NFF, TB], BF16)
            actR = Res("act")
            u1p = Pool(kb, sc, "u1", [128, nseg, Lb + 2], F32, 4)
            gfp = Pool(kb, sc, "gf", [128, TB], F32, 4)
            u1_tiles = [u1p.next() for _ in range(4)]
            u1_i = [0]
            if not isS:
                for u1, u1R in u1_tiles:
                    kb.V(lambda e, u1=u1: e.memset(u1[:, :, 0:1], 0.0), writes=[u1R])
                    kb.V(lambda e, u1=u1: e.memset(u1[:, :, Lb + 1:Lb + 2], 0.0), writes=[u1R])

            def ffn_chunk(b, s, e_, w4, wr, c, cc):
                u1, u1R = u1_tiles[u1_i[0] % 4]
                u1_i[0] += 1
                lhs = w4[:, 0, :, cc * 128:(cc + 1) * 128]
                if isS:
                    c0 = max(0, s - 1)
                    c1 = min(T, e_ + 1)
                    if s == 0:
                        kb.V(lambda e: e.memset(u1[:, 0, 0:1], 0.0), writes=[u1R])
                    if e_ == T:
                        kb.V(lambda e: e.memset(u1[:, 0, Lb + 1:Lb + 2], 0.0), writes=[u1R])
                    pos = c0
                    while pos < c1:
                        w = min(512, c1 - pos)
                        pt, pr = self.psum.next()
                        for kc in range(NCH):
                            kb.mm(pt[:, 0:w], lhs[:, kc, :], h2[:, kc, pos:pos + w], start=(kc == 0),
                                  stop=(kc == NCH - 1), reads=[wr, hR], writes=[pr])
                        dc = pos - (s - 1)
                        kb.A(lambda e, pt=pt, dc=dc, w=w: e.activation(out=u1[:, 0, dc:dc + w], in_=pt[:, 0:w],
                                                                       func=AF.Copy), reads=[pr], writes=[u1R])
                        pos += w
                else:
                    pt, pr = self.psum.next()
                    for kc in range(NCH):
                        kb.mm(pt[:], lhs[:, kc, :], h2[:, kc, s:e_], start=(kc == 0), stop=(kc == NCH - 1),
                              reads=[wr, hR], writes=[pr])
                    kb.A(lambda e: e.activation(out=u1[:, :, 1:1 + Lb], in_=pt[:].rearrange("p (s t) -> p s t", s=nseg),
                                                func=AF.Copy), reads=[pr], writes=[u1R])
                p2, p2r = self.pacc.next()
                for kc in range(NCH):
                    kb.mm(p2[:], w4[:, 1, kc, cc * 128:(cc + 1) * 128], h2[:, kc, s:e_], start=(kc == 0),
                          stop=(kc == NCH - 1), reads=[wr, hR], writes=[p2r])
                yield
                gf, gfR = gfp.next()
                go = gf[:].rearrange("p (s t) -> p s t", s=nseg)
                kb.A(lambda e: e.activation(out=go, in_=u1[:, :, 0:Lb], func=AF.Identity,
                                            scale=self.vcol(l, "fconv0", c), bias=self.vcol(l, "fconv_b", c)),
                     reads=[u1R, Rc], writes=[gfR])
                yield
                for k in range(1, 3):
                    kb.V(lambda e, k=k: e.scalar_tensor_tensor(out=go, in0=u1[:, :, k:k + Lb],
                                                               scalar=self.vcol(l, "fconv%d" % k, c), in1=go,
                                                               op0=ALU.mult, op1=ALU.add), reads=[u1R, gfR, Rc], writes=[gfR])
                    yield
                kb.A(lambda e: e.activation(out=gf[:], in_=gf[:], func=AF.Gelu_apprx_tanh), reads=[gfR], writes=[gfR])
                yield
                kb.V(lambda e: e.tensor_tensor(out=act[:, c, :], in0=gf[:], in1=p2[:], op=ALU.mult),
                     reads=[gfR, p2r], writes=[actR])
                yield

            for b in range(nblk):
                s, e_ = b * TB, (b + 1) * TB
                for grp in range(11):
                    wt, wr = self.wload(d["w_up"][l, grp], 4096)
                    w4 = wt[:, 0:4096].rearrange("p (u k n) -> p u k n", u=2, k=NCH)
                    interleave([ffn_chunk(b, s, e_, w4, wr, grp * 2 + cc, cc) for cc in range(2)])
                for st in range(8):
                    wt, wr = self.wload(d["w_dn"][l, st], 2816)
                    w3 = wt[:, 0:2816].rearrange("p (k n) -> p k n", k=NFF)
                    for cc in range(1):
                        m = st
                        pt, pr = self.psum.next()
                        for kc in range(NFF):
                            kb.mm(pt[:], w3[:, kc, :], act[:, kc, :], start=(kc == 0),
                                  stop=(kc == NFF - 1), reads=[wr, actR], writes=[pr])
                        kb.V(lambda e, pt=pt, m=m: e.scalar_tensor_tensor(out=x[:, m, s:e_], in0=pt[:],
                                                                          scalar=self.par[:, l, g, 5, m:m + 1],
                                                                          in1=x[:, m, s:e_], op0=ALU.mult, op1=ALU.add),
                             reads=[pr, Rc, xR[b]], writes=[xR[b]])
                if self.dbg == (G, l, b):
                    self.dump("act", act[:], [128, NFF, TB], [actR])
                    self.dump("xffn", x[:, :, s:e_], [128, NCH, TB], [xR[b]])
                    kb.retire()
            kb.retire()


def _tile_w(W, cw):
    K, N = W.shape
    a = W.reshape(K // 128, 128, N // cw, cw)
    return np.ascontiguousarray(a.transpose(2, 1, 0, 3)).reshape(N // cw, 128, (K // 128) * cw)


def host_consts(cfg):
    TS = cfg.ts
    t = np.arange(TS)
    row = (t // cfg.grid_w).astype(np.float32)
    col = (t % cfg.grid_w).astype(np.float32)
    nf = 32
    freqs = (10000.0 ** (-np.arange(nf, dtype=np.float32) / nf)).astype(np.float32)
    cos = np.zeros((128, TS), np.float32)
    sin = np.zeros((128, TS), np.float32)
    ang_r = (row[None, :] * freqs[:, None]).astype(np.float32)
    ang_c = (col[None, :] * freqs[:, None]).astype(np.float32)
    cos[0:32] = np.cos(ang_r); cos[32:64] = np.cos(ang_r)
    cos[64:96] = np.cos(ang_c); cos[96:128] = np.cos(ang_c)
    sin[0:32] = -np.sin(ang_r); sin[32:64] = np.sin(ang_r)
    sin[64:96] = -np.sin(ang_c); sin[96:128] = np.sin(ang_c)
    rope = np.stack([cos, sin]).astype(np.float32)
    perm = np.zeros((128, 128), np.float32)
    for m in range(128):
        base = (m // 64) * 64
        i = m - base
        partner = base + (i + 32) % 64
        perm[partner, m] = 1.0
    kk = np.arange(128)[:, None]
    qq = np.arange(128)[None, :]
    mL = (qq <= kk).astype(np.float32)
    mR = (kk <= qq).astype(np.float32)
    masks = np.concatenate([np.tile(mL, (1, 4)), np.tile(mR, (1, 4))], axis=1).astype(np.float32)
    pedge = np.zeros((128, 64), np.float32)
    for gi in range(4):
        win = 2 << gi
        for j in range(8):
            cl = min(j + win // 2, win)
            cr = min(j + 1 + win // 2, win)
            pedge[:, gi * 16 + j] = win / cl
            pedge[:, gi * 16 + 8 + (7 - j)] = win / cr
    return rope, perm, masks, pedge


def prep_shared(inp, cfg):
    L = cfg.depth
    sh = {}
    vecs = np.zeros((L, 128, NV), np.float32)

    def put(l, name, v):
        o, w = VEC_OFF[name]
        vecs[l, :, o:o + w] = _fm(v)

    for l in range(L):
        put(l, "norm1", inp["norm1"][l]); put(l, "norm2", inp["norm2"][l]); put(l, "b_ada", inp["b_ada"][l])
        for k in range(4):
            put(l, "conv%d" % k, inp["lru_conv"][l, k])
        put(l, "conv_b", inp["lru_conv_b"][l])
        for dr in range(2):
            put(l, "ba%d" % dr, inp["lru_ba"][l, dr]); put(l, "bx%d" % dr, inp["lru_bx"][l, dr])
            put(l, "lam%d" % dr, inp["lru_lambda"][l, dr])
        put(l, "b_gate", inp["b_gate"][l]); put(l, "pool_scale", inp["pool_scale"][l])
        for k in range(3):
            put(l, "fconv%d" % k, inp["ffn_conv"][l, k])
        put(l, "fconv_b", inp["ffn_conv_b"][l])
        put(l, "final_norm", inp["final_norm"])
    sh["vecs"] = vecs
    sh["w_ada"] = np.stack([_tile_w(np.asarray(inp["w_ada"][l]), 512) for l in range(L)])
    sh["w_in"] = np.stack([_tile_w(np.asarray(inp["w_in"][l][:, 0:3584]), 512) for l in range(L)])
    wg = np.zeros((L, 8, 128, 3072), np.float32)
    wb = np.zeros((L, 8, 128, 3072), np.float32)
    for l in range(L):
        G = np.asarray(inp["w_in"][l][:, 3584:]).reshape(8, 128, 3, 8, 128)
        wg[l] = G.transpose(3, 1, 2, 0, 4).reshape(8, 128, 3072)
        Bm = np.asarray(inp["w_branch"][l]).reshape(3, 8, 128, 8, 128)
        wb[l] = Bm.transpose(3, 2, 0, 1, 4).reshape(8, 128, 3072)
    sh["w_g"] = wg
    sh["w_br"] = wb
    sh["w_out"] = np.stack([_tile_w(np.asarray(inp["w_out"][l]), 512) for l in range(L)])
    wu = np.zeros((L, 11, 128, 4096), np.float32)
    for l in range(L):
        U = np.asarray(inp["ffn_up"][l]).reshape(8, 128, 2, 11, 256)
        wu[l] = U.transpose(3, 1, 2, 0, 4).reshape(11, 128, 4096)
    sh["w_up"] = wu
    sh["w_dn"] = np.stack([_tile_w(np.asarray(inp["ffn_down"][l]), 128) for l in range(L)])
    wl = np.zeros((L, 8, 128, 512), np.float32)
    for l in range(L):
        A = np.stack([np.asarray(inp["lru_wa"][l]), np.asarray(inp["lru_wx"][l])], axis=0)
        wl[l] = A.transpose(2, 3, 1, 0, 4).reshape(8, 128, 512)
    sh["w_lru"] = wl
    wp = np.zeros((L, 4, 128, 512), np.float32)
    for l in range(L):
        Pw = np.asarray(inp["pool_w"][l]).reshape(4, 2, 128, 256)
        wp[l] = Pw.transpose(0, 2, 1, 3).reshape(4, 128, 512)
    sh["w_pool"] = wp
    sh["sink"] = np.ascontiguousarray(np.broadcast_to(np.asarray(inp["attn_sink"])[:, None, :], (L, 128, 8))).astype(np.float32)
    rope, perm, masks, pedge = host_consts(cfg)
    sh["rope"] = rope; sh["perm"] = perm; sh["masks"] = masks; sh["pedge"] = pedge
    return sh


def prep_core(inp, cfg, core, sh):
    L = cfg.depth
    npq = cfg.np_seq
    m = dict(sh)
    xp = np.asarray(inp["x_prompt"][core * npq:(core + 1) * npq]).reshape(cfg.tp, D)
    m["xp"] = np.ascontiguousarray(xp.T)
    n_s = inp["x_sample"].shape[0]
    sb = min(core // (8 // n_s), n_s - 1) if n_s <= 8 else 0
    m["xs"] = np.ascontiguousarray(np.asarray(inp["x_sample"][sb]).T)
    cv = np.zeros((128, NCH, 2), np.float32)
    cv[:, :, 0] = _fm(inp["c_ctx"])
    cv[:, :, 1] = _fm(inp["c"][sb])
    m["cvec"] = cv
    h0 = np.zeros((L, 128, 2, NCH), np.float32)
    for l in range(L):
        for dr in range(2):
            h0[l, :, dr, :] = _fm(inp["state_lru"][sb, l, dr])
    m["h0s"] = h0
    ck = np.asarray(inp["cache_k"][sb])
    m["kcT"] = np.ascontiguousarray(ck.transpose(0, 3, 2, 1)).reshape(L, 128, 2 * cfg.past)
    cvv = np.asarray(inp["cache_v"][sb]).reshape(L, cfg.past // 128, 128, 256)
    m["vc"] = np.ascontiguousarray(cvv.transpose(0, 2, 1, 3)).reshape(L, 128, (cfg.past // 128) * 256)
    return m, sb


_NC_CACHE = {}


def run(inp, cfg, ncores=8):
    key = (cfg.np_seq, cfg.lp, cfg.ts, cfg.past, cfg.grid_w, cfg.depth)
    if key not in _NC_CACHE:
        _NC_CACHE[key] = Builder(cfg).build()
    nc = _NC_CACHE[key]
    sh = prep_shared(inp, cfg)
    maps, sbs = [], []
    for core in range(ncores):
        m, sb = prep_core(inp, cfg, core, sh)
        maps.append(m)
        sbs.append(sb)
    res = run_bass_kernel_spmd(nc, maps, core_ids=list(range(ncores)))
    return res, sbs


def assemble(res, sbs, cfg, n_prompt, n_sample, ncores=8):
    L = cfg.depth
    npq = cfg.np_seq
    yp = np.zeros((n_prompt, cfg.lp, D), np.float32)
    ys = np.zeros((n_sample, cfg.ts, D), np.float32)
    nk = np.zeros((n_prompt, L, cfg.lp, 2, 128), np.float32)
    nv = np.zeros((n_prompt, L, cfg.lp, 2, 128), np.float32)
    nh = np.zeros((n_prompt, L, 2, D), np.float32)
    done = set()
    for core in range(ncores):
        r = res.results[core]
        yp[core * npq:(core + 1) * npq] = r["yp"].T.reshape(npq, cfg.lp, D)
        if sbs[core] not in done:
            ys[sbs[core]] = r["ys"].T
            done.add(sbs[core])
        nk[core * npq:(core + 1) * npq] = r["ko"].reshape(L, npq, cfg.lp, 2, 128).transpose(1, 0, 2, 3, 4)
        nv[core * npq:(core + 1) * npq] = r["vo"].reshape(L, npq, cfg.lp, 2, 128).transpose(1, 0, 2, 3, 4)
        ho = r["ho"].reshape(128, L, 2, NCH, npq)
        nh[core * npq:(core + 1) * npq] = ho.transpose(4, 1, 2, 3, 0).reshape(npq, L, 2, D)
    return yp, ys, nk, nv, nh


def kernel(**inputs):
    cfg = CFG()
    inp = {k: np.asarray(v) for k, v in inputs.items()}
    res, sbs = run(inp, cfg)
    return assemble(res, sbs, cfg, inp["x_prompt"].shape[0], inp["x_sample"].shape[0])
```
